# Optimizing a Trainium2 kernel written in Bass

```python
import jax, jax.numpy as jnp
from jax import lax
import numpy as np

D_MODEL = 1024
BATCH = 16
SEQ = 256
DEPTH = 4
DEC_BATCH = 2
DEC_SEQ = 1024
PAST_LEN = 256

GRID_W = 64
D_CONV = D_MODEL // 2
CONV_WIDTH = 31
HEAD_DIM = 64
N_HEADS_NA = D_MODEL // 128
D_NA = N_HEADS_NA * HEAD_DIM
WIN_H_MAX = 8
WIN_W = 16
QB_W = 16
KB_W = 2 * WIN_W
ROPE_BASE = 10000.0
Q_BLOCK = 128
D_GM = D_MODEL // 2
GM_CHUNK = 128
GM_GROUPS = 4
GM_CH = D_GM // GM_GROUPS
N_BRANCH = 3
D_IN = 2 * D_CONV + 3 * D_NA + 2 * D_GM + N_BRANCH * D_MODEL
N_EGROUPS = 4
EXP_PER_GROUP = 8
N_EXPERTS = N_EGROUPS * EXP_PER_GROUP
TOP_K = 2
D_EXPERT = D_MODEL // 8
ALPHA = (2 * DEPTH) ** 0.25
BETA = (8 * DEPTH) ** -0.25
LN_EPS = 1e-5
NEG_INF = -1e30

kernel_name = "hybrid_diffusion_conv_natten_gmlp_hmoe_step"


def layer_norm(x, g, b):
    xf = x.astype(jnp.float32)
    mu = jnp.mean(xf, axis=-1, keepdims=True)
    var = jnp.mean(jnp.square(xf - mu), axis=-1, keepdims=True)
    return ((xf - mu) * lax.rsqrt(var + LN_EPS)).astype(x.dtype) * g + b


def rope_axis(x, pos):
    n = x.shape[-1]
    inv = ROPE_BASE ** (-jnp.arange(0, n, 2, dtype=jnp.float32) / n)
    ang = pos[:, None] * inv[None, :]
    cos = jnp.cos(ang)[None, :, None, :]
    sin = jnp.sin(ang)[None, :, None, :]
    x1, x2 = x[..., : n // 2], x[..., n // 2:]
    return jnp.concatenate([x1 * cos - x2 * sin, x1 * sin + x2 * cos], axis=-1)


def axial_rope(x):
    L = x.shape[1]
    t = jnp.arange(L)
    pr = (t // GRID_W).astype(jnp.float32)
    pc = (t % GRID_W).astype(jnp.float32)
    half = HEAD_DIM // 2
    out = jnp.concatenate([rope_axis(x[..., :half], pr), rope_axis(x[..., half:], pc)], axis=-1)
    return out.astype(x.dtype)


def conformer_conv(a, b, conv_dw, conv_b, ln_g, ln_b, conv_pw):
    y = a * jax.nn.sigmoid(b)
    y = lax.conv_general_dilated(
        y, conv_dw[:, None, :], window_strides=(1,),
        padding=[(CONV_WIDTH // 2, CONV_WIDTH // 2)],
        dimension_numbers=("NWC", "WIO", "NWC"),
        feature_group_count=D_CONV) + conv_b
    y = jax.nn.silu(layer_norm(y, ln_g, ln_b))
    return y @ conv_pw


def spatial_gating(u, v, ln_g, ln_b, ws, bs, out):
    B, L, _ = u.shape
    n = L // GM_CHUNK
    u = jax.nn.gelu(u)
    v = layer_norm(jax.nn.gelu(v), ln_g, ln_b).reshape(B, n, GM_CHUNK, GM_GROUPS, GM_CH)
    sv = jnp.einsum('gpq,bnqgc->bnpgc', ws, v) + bs.T[:, :, None]
    return (u * sv.reshape(B, L, D_GM)) @ out


def context_attention(q, k, v):
    B, L, H, hd = q.shape
    nb = L // Q_BLOCK
    scale = hd ** -0.5
    qb = q.reshape(B, nb, Q_BLOCK, H, hd).transpose(1, 0, 2, 3, 4)

    def one_block(qi):
        s = jnp.einsum('bqhd,bkhd->bhqk', qi, k).astype(jnp.float32) * scale
        p = jax.nn.softmax(s, axis=-1).astype(v.dtype)
        return jnp.einsum('bhqk,bkhd->bqhd', p, v)

    o = lax.map(one_block, qb)
    return o.transpose(1, 0, 2, 3, 4).reshape(B, L, H * hd)


def neighbourhood_attention(q, k, v, ck, cv, rpb):
    B, L, H, hd = q.shape
    rows = L // GRID_W
    wh = min(WIN_H_MAX, rows)
    ncb = GRID_W // QB_W
    scale = hd ** -0.5
    kg = k.reshape(B, rows, GRID_W, H, hd)
    vg = v.reshape(B, rows, GRID_W, H, hd)
    qg = q.reshape(B, rows, ncb, QB_W, H, hd)
    r_all = jnp.arange(rows)
    row_start = jnp.clip(r_all - wh // 2, 0, rows - wh)
    row_idx = row_start[:, None] + jnp.arange(wh)
    j = jnp.arange(ncb)
    kb_start = jnp.clip(j * QB_W - WIN_W // 2, 0, GRID_W - KB_W)
    col_idx = kb_start[:, None] + jnp.arange(KB_W)
    qcol = j[:, None] * QB_W + jnp.arange(QB_W)
    col_start = jnp.clip(qcol - WIN_W // 2, 0, GRID_W - WIN_W)
    kc = col_idx[:, None, :]
    col_valid = (kc >= col_start[..., None]) & (kc < col_start[..., None] + WIN_W)
    col_mask = jnp.where(col_valid, 0.0, NEG_INF).astype(jnp.float32)
    dc_idx = jnp.clip(kc - qcol[..., None] + WIN_W - 1, 0, 2 * WIN_W - 2)
    n_loc = wh * KB_W

    def per_row(args):
        qr, ridx, r = args
        kr = kg[:, ridx][:, :, col_idx]
        vr = vg[:, ridx][:, :, col_idx]
        s_loc = jnp.einsum('bjqhd,bwjkhd->bhjqwk', qr, kr).astype(jnp.float32) * scale
        bias = rpb[:, ridx - r + WIN_H_MAX - 1][:, :, dc_idx]
        s_loc = s_loc + bias.transpose(0, 2, 3, 1, 4)[None] + col_mask[:, :, None, :]
        s_ctx = jnp.einsum('bjqhd,bchd->bhjqc', qr, ck).astype(jnp.float32) * scale
        s = jnp.concatenate([s_loc.reshape(B, H, ncb, QB_W, n_loc), s_ctx], axis=-1)
        p = jax.nn.softmax(s, axis=-1).astype(v.dtype)
        p_loc = p[..., :n_loc].reshape(B, H, ncb, QB_W, wh, KB_W)
        p_ctx = p[..., n_loc:]
        return (jnp.einsum('bhjqwk,bwjkhd->bjqhd', p_loc, vr)
                + jnp.einsum('bhjqc,bchd->bjqhd', p_ctx, cv))

    o = lax.map(per_row, (qg.transpose(1, 0, 2, 3, 4, 5), row_idx, r_all))
    return o.transpose(1, 0, 2, 3, 4, 5).reshape(B, L, H * hd)


def token_mixer(h, w_in, conv_dw, conv_b, conv_ln_g, conv_ln_b, conv_pw, na_rpb, na_out,
                gm_ln_g, gm_ln_b, gm_ws, gm_bs, gm_out, w_o, ctx_k, ctx_v):
    B, L, _ = h.shape
    z = h @ w_in
    cuts = [D_CONV, 2 * D_CONV, 2 * D_CONV + D_NA, 2 * D_CONV + 2 * D_NA,
            2 * D_CONV + 3 * D_NA, 2 * D_CONV + 3 * D_NA + D_GM,
            2 * D_CONV + 3 * D_NA + 2 * D_GM]
    ca, cb, q, k, v, gu, gv, gz = jnp.split(z, cuts, axis=-1)
    q = q.reshape(B, L, N_HEADS_NA, HEAD_DIM)
    k = k.reshape(B, L, N_HEADS_NA, HEAD_DIM)
    v = v.reshape(B, L, N_HEADS_NA, HEAD_DIM)
    br_conv = conformer_conv(ca, cb, conv_dw, conv_b, conv_ln_g, conv_ln_b, conv_pw)
    if ctx_k is None:
        att = context_attention(q, k, v)
    else:
        att = neighbourhood_attention(axial_rope(q), axial_rope(k), v, ctx_k, ctx_v, na_rpb)
    br_att = att @ na_out
    br_gm = spatial_gating(gu, gv, gm_ln_g, gm_ln_b, gm_ws, gm_bs, gm_out)
    g = jax.nn.sigmoid(gz).reshape(B, L, N_BRANCH, D_MODEL)
    merged = g[:, :, 0] * br_conv + g[:, :, 1] * br_att + g[:, :, 2] * br_gm
    return merged @ w_o, k, v


def hier_moe(h, rg_w, rg_b, re_w, re_b, w1, w3, w2):
    B, L, D = h.shape
    t = h.reshape(B * L, D)
    g_logits = (t @ rg_w + rg_b).astype(jnp.float32)
    g_prob = jax.nn.softmax(g_logits, axis=-1)
    g_idx = jnp.argmax(g_logits, axis=-1)
    g_p = jnp.take_along_axis(g_prob, g_idx[:, None], axis=-1)
    e_logits = (jnp.einsum('td,gde->tge', t, re_w) + re_b).astype(jnp.float32)
    e_sel = jnp.take_along_axis(e_logits, g_idx[:, None, None], axis=1)[:, 0]
    top_v, top_i = lax.top_k(e_sel, TOP_K)
    top_w = jax.nn.softmax(top_v, axis=-1) * g_p
    expert_id = g_idx[:, None] * EXP_PER_GROUP + top_i
    gate = jnp.sum(jax.nn.one_hot(expert_id, N_EXPERTS, dtype=jnp.float32) * top_w[..., None], axis=1)
    hid = jax.nn.silu(jnp.einsum('td,edf->tef', t, w1)) * jnp.einsum('td,edf->tef', t, w3)
    hid = hid * gate[..., None].astype(hid.dtype)
    return jnp.einsum('tef,efd->td', hid, w2).reshape(B, L, D)


def trunk_layer(x, cvec, w_ada, b_ada, mix_params, ln1_g, ln1_b, moe_params, ln2_g, ln2_b,
                ctx_k, ctx_v):
    m = (jax.nn.silu(cvec) @ w_ada + b_ada)[:, None, :]
    sh1, sc1, g1, sh2, sc2, g2 = jnp.split(m, 6, axis=-1)
    mix, k, v = token_mixer(x * (1 + sc1) + sh1, *mix_params, ctx_k, ctx_v)
    x = layer_norm(ALPHA * x + g1 * mix, ln1_g, ln1_b)
    y = hier_moe(x * (1 + sc2) + sh2, *moe_params)
    x = layer_norm(ALPHA * x + g2 * y, ln2_g, ln2_b)
    return x, k, v


def setup_inputs(seed: int = 0) -> dict:
    key = jax.random.key(seed)
    keys = jax.random.split(key, 48)
    counter = [0]

    def nrm(shape, s):
        kk = keys[counter[0]]
        counter[0] += 1
        return jax.random.normal(kk, shape, jnp.float32) * s

    L, D = DEPTH, D_MODEL
    return {
        "x_prompt": nrm((BATCH, SEQ, D), 1.0),
        "x_sample": nrm((DEC_BATCH, DEC_SEQ, D), 1.0),
        "cache_na_k": nrm((DEC_BATCH, DEPTH, PAST_LEN, N_HEADS_NA, HEAD_DIM), 1.0),
        "cache_na_v": nrm((DEC_BATCH, DEPTH, PAST_LEN, N_HEADS_NA, HEAD_DIM), 1.0),
        "c": nrm((DEC_BATCH, D), 1.0),
        "c_ctx": nrm((D,), 1.0),
        "w_ada": nrm((L, D, 6 * D), 0.2 * D ** -0.5),
        "b_ada": nrm((L, 6 * D), 0.01),
        "w_in": nrm((L, D, D_IN), D ** -0.5),
        "conv_dw": nrm((L, CONV_WIDTH, D_CONV), CONV_WIDTH ** -0.5),
        "conv_b": nrm((L, D_CONV), 0.01),
        "conv_ln_g": 1.0 + nrm((L, D_CONV), 0.02),
        "conv_ln_b": nrm((L, D_CONV), 0.01),
        "conv_pw": nrm((L, D_CONV, D), D_CONV ** -0.5),
        "na_rpb": nrm((L, N_HEADS_NA, 2 * WIN_H_MAX - 1, 2 * WIN_W - 1), 0.1),
        "na_out": nrm((L, D_NA, D), D_NA ** -0.5),
        "gm_ln_g": 1.0 + nrm((L, D_GM), 0.02),
        "gm_ln_b": nrm((L, D_GM), 0.01),
        "gm_ws": nrm((L, GM_GROUPS, GM_CHUNK, GM_CHUNK), GM_CHUNK ** -0.5),
        "gm_bs": 1.0 + nrm((L, GM_GROUPS, GM_CHUNK), 0.01),
        "gm_out": nrm((L, D_GM, D), D_GM ** -0.5),
        "w_o": nrm((L, D, D), BETA * D ** -0.5),
        "ln1_g": 1.0 + nrm((L, D), 0.02),
        "ln1_b": nrm((L, D), 0.01),
        "rg_w": nrm((L, D, N_EGROUPS), D ** -0.5),
        "rg_b": nrm((L, N_EGROUPS), 0.01),
        "re_w": nrm((L, N_EGROUPS, D, EXP_PER_GROUP), D ** -0.5),
        "re_b": nrm((L, N_EGROUPS, EXP_PER_GROUP), 0.01),
        "moe_w1": nrm((L, N_EXPERTS, D, D_EXPERT), D ** -0.5),
        "moe_w3": nrm((L, N_EXPERTS, D, D_EXPERT), D ** -0.5),
        "moe_w2": nrm((L, N_EXPERTS, D_EXPERT, D), BETA * D_EXPERT ** -0.5),
        "ln2_g": 1.0 + nrm((L, D), 0.02),
        "ln2_b": nrm((L, D), 0.01),
    }


def reference(x_prompt, x_sample, cache_na_k, cache_na_v, c, c_ctx, w_ada, b_ada, w_in,
              conv_dw, conv_b, conv_ln_g, conv_ln_b, conv_pw, na_rpb, na_out,
              gm_ln_g, gm_ln_b, gm_ws, gm_bs, gm_out, w_o, ln1_g, ln1_b,
              rg_w, rg_b, re_w, re_b, moe_w1, moe_w3, moe_w2, ln2_g, ln2_b):
    xp, xs = x_prompt, x_sample
    new_k, new_v = [], []
    for i in range(DEPTH):
        mix_params = (w_in[i], conv_dw[i], conv_b[i], conv_ln_g[i], conv_ln_b[i], conv_pw[i],
                      na_rpb[i], na_out[i], gm_ln_g[i], gm_ln_b[i], gm_ws[i], gm_bs[i],
                      gm_out[i], w_o[i])
        moe_params = (rg_w[i], rg_b[i], re_w[i], re_b[i], moe_w1[i], moe_w3[i], moe_w2[i])
        xp, kp, vp = trunk_layer(xp, c_ctx[None], w_ada[i], b_ada[i], mix_params,
                                 ln1_g[i], ln1_b[i], moe_params, ln2_g[i], ln2_b[i], None, None)
        new_k.append(kp)
        new_v.append(vp)
        xs, _, _ = trunk_layer(xs, c, w_ada[i], b_ada[i], mix_params,
                               ln1_g[i], ln1_b[i], moe_params, ln2_g[i], ln2_b[i],
                               cache_na_k[:, i], cache_na_v[:, i])
    new_na_k = jnp.stack(new_k, axis=1)
    new_na_v = jnp.stack(new_v, axis=1)
    return (xp, xs, new_na_k, new_na_v)
```

```python
import os
import numpy as np
import concourse.bass as bass
import concourse.mybir as mybir
from concourse.bass_utils import run_bass_kernel_spmd

F32 = mybir.dt.float32
BF16 = mybir.dt.bfloat16
AF = mybir.ActivationFunctionType
ALU = mybir.AluOpType
AX = mybir.AxisListType

D = 1024
NT = 1024
DEPTH = 4
D_IN = 6656
ALPHA = (2 * DEPTH) ** 0.25
LN_EPS = 1e-5
NEG = -1e30
NPV = 216
NSLOT = 17
GELU_C = 0.7978845608028654

C_CA, C_CB, C_Q, C_K, C_V, C_GU, C_GV, C_GZ = 0, 512, 1024, 1536, 2048, 2560, 3072, 3584


class Prog:
    def __init__(self, nc, sems):
        self.nc = nc
        self.engs = {"pe": nc.tensor, "act": nc.scalar, "dve": nc.vector, "pool": nc.gpsimd, "sp": nc.sync}
        self.sem = {e: sems[i] for i, e in enumerate(self.engs)}
        self.free_sems = list(sems[len(self.engs):])
        self.cnt = {e: 0 for e in self.engs}
        self.seen = {e: {} for e in self.engs}
        self.last_w = {}
        self.readers = {}
        self.dsem = {}

    def _wait(self, eng, tok):
        s, v = tok
        if s == eng and eng == "pe":
            return
        if self.seen[eng].get(s, 0) >= v:
            return
        if s in self.sem:
            h = self.sem[s]
        else:
            h, v = self.dsem[s][0], self.dsem[s][1]
        self.engs[eng].wait_ge(h, v)
        self.seen[eng][s] = v

    def _deps(self, eng, r, w, nowaw=False):
        for k in r:
            t = self.last_w.get(k)
            if t is not None:
                self._wait(eng, t)
        for k in w:
            t = self.last_w.get(k)
            if t is not None and not nowaw:
                self._wait(eng, t)
            for s, v in self.readers.get(k, {}).items():
                self._wait(eng, (s, v))

    def _record(self, tok, r, w):
        s, v = tok
        for k in r:
            d = self.readers.setdefault(k, {})
            d[s] = max(d.get(s, 0), v)
        for k in w:
            self.last_w[k] = tok
            self.readers[k] = {}

    def op(self, eng, fn, r=(), w=()):
        self._deps(eng, r, w)
        ins = fn(self.engs[eng])
        self.cnt[eng] += 1
        ins.then_inc(self.sem[eng], 1)
        self._record((eng, self.cnt[eng]), r, w)

    def dma(self, q, slot, out, in_, r=(), w=()):
        if slot not in self.dsem:
            self.dsem[slot] = [self.free_sems.pop(), 0]
        self._deps(q, r, w, nowaw=True)
        ins = self.engs[q].dma_start(out=out, in_=in_)
        self.dsem[slot][1] += 16
        ins.then_inc(self.dsem[slot][0], 16)
        self._record((slot, self.dsem[slot][1]), r, w)

    def alias(self, old_keys, new_keys):
        acc = {}
        for k in old_keys:
            t = self.last_w.get(k)
            if t is not None:
                acc[t[0]] = max(acc.get(t[0], 0), t[1])
            for s_, v in self.readers.get(k, {}).items():
                acc[s_] = max(acc.get(s_, 0), v)
        for k in new_keys:
            d = self.readers.setdefault(k, {})
            for s_, v in acc.items():
                d[s_] = max(d.get(s_, 0), v)

    def finish(self):
        for slot, (h, v) in self.dsem.items():
            if v:
                self.engs["sp"].wait_ge(h, v)
        for e in self.engs:
            if e != "sp" and self.cnt[e]:
                self.engs["sp"].wait_ge(self.sem[e], self.cnt[e])


def _slot_index(j, m, b):
    r = 2 * j + b
    rs = min(max(r - 4, 0), 8)
    v0 = rs <= 2 * m < rs + 8
    v1 = rs <= 2 * m + 1 < rs + 8
    dr0 = 2 * m - r
    if v0 and v1:
        return 13 - (dr0 + 7)
    if (not v0) and v1:
        assert dr0 + 1 == -4
        return 14
    if v0 and not v1:
        assert dr0 == 3
        return 15
    return 16


def _local_chunks(j):
    ms = set()
    for b in range(2):
        r = 2 * j + b
        rs = min(max(r - 4, 0), 8)
        for rr in range(rs, rs + 8):
            ms.add(rr // 2)
    return sorted(ms)


class _Stop(Exception):
    pass


def build_nc(nlayers=DEPTH, dbg=(), stop=None):
    nc = bass.Bass("TRN2", target_bir_lowering=False)

    def din(name, shape):
        return nc.dram_tensor(name, list(shape), F32, kind="ExternalInput").ap()

    def dout(name, shape):
        return nc.dram_tensor(name, list(shape), F32, kind="ExternalOutput").ap()

    x_d = din("x", [NT, D])
    cvec_d = din("cvec", [128, 8])
    ctxk_d = din("ctxk", [DEPTH, 256, 512])
    ctxv_d = din("ctxv", [DEPTH, 256, 512])
    w_ada_d = din("w_ada", [DEPTH, D, 6 * D])
    w_in_d = din("w_in", [DEPTH, D, D_IN])
    conv_pw_d = din("conv_pw", [DEPTH, 512, D])
    na_out_d = din("na_out", [DEPTH, 512, D])
    gm_out_d = din("gm_out", [DEPTH, 512, D])
    w_o_d = din("w_o", [DEPTH, D, D])
    w1_d = din("moe_w1", [DEPTH, 32, D, 128])
    w3_d = din("moe_w3", [DEPTH, 32, D, 128])
    w2_d = din("moe_w2", [DEPTH, 32, 128, D])
    pv_d = din("pv", [DEPTH, 128, NPV])
    rw_d = din("rw", [DEPTH, 128, 8 * 36])
    rb_d = din("rb", [DEPTH, 128, 36])
    gmw_d = din("gmw", [DEPTH, 128, 512])
    gmbs_d = din("gmbs", [DEPTH, 128, 512])
    gmln_d = din("gmln", [DEPTH, 128, 1024])
    slots_d = din("slots", [DEPTH, 128, 8 * NSLOT * 64])
    rope_d = din("rope", [128, 2 * 8 * 64])
    cf_d = din("cf", [128, 3 * 128 + 3])
    cb_d = din("cb", [128, 128 + 32 * 128])
    crow_d = din("crow", [1, 256])
    y_d = dout("y", [NT, D])
    nk_d = dout("newk", [DEPTH, NT, 512])
    nv_d = dout("newv", [DEPTH, NT, 512])
    dbg_d = {name: dout("dbg_" + name, shape) for name, shape in dbg}

    from contextlib import ExitStack
    with ExitStack() as es:
        def sb(name, shape, dt=F32):
            return es.enter_context(nc.sbuf_tensor("sb_" + name, list(shape), dt))

        sems = [es.enter_context(nc.semaphore("s%d" % i)) for i in range(40)]
        P = Prog(nc, sems)

        xT = sb("xT", [128, 8, NT])
        hT = sb("hT", [128, 8, NT], BF16)
        NW = 3
        wpool = sb("wpool", [128, NW, 4096], BF16)
        big = sb("big", [128, 8192], F32)
        S1 = sb("S1", [128, 4096], F32)
        S2 = sb("S2", [128, 4 * 4 * 286], F32)
        S3 = sb("S3", [128, 8 * 8 * 65], BF16)
        attT = sb("attT", [128, 4, NT], BF16)
        rope = sb("rope", [128, 2, 8, 64])
        cf = sb("cf", [128, 3 * 128 + 3])
        cbb = sb("cbb", [128, 128 + 32 * 128], BF16)
        crow = sb("crow", [1, 256], BF16)
        pv = sb("pv", [128, NPV])
        modv = sb("modv", [128, 104])
        scol = sb("scol", [128, 8], BF16)
        cvs = sb("cvs", [128, 8])
        rwb = sb("rwb", [128, 8, 36], BF16)
        rbb = sb("rbb", [128, 36])
        gmw = sb("gmw", [128, 4, 128], BF16)
        gmbs = sb("gmbs", [128, 4, 128])
        gmln = sb("gmln", [128, 2, 512])
        kcT = sb("kcT", [128, 4, 256], BF16)
        vca = sb("vca", [128, 2, 8, 65], BF16)
        PT = sb("PT", [128, 2, 8, 128], BF16)
        atm = sb("atm", [128, 2, 512], BF16)
        rcp = sb("rcp", [128, 2, 8])
        stg = sb("stg", [128, 4, 512])
        stg2 = sb("stg2", [128, 2, 512])
        gtm = sb("gtm", [128, 8, 32])
        gateT = sb("gateT", [32, NT], BF16)
        bnst = sb("bnst", [128, 8])
        modraw = sb("modraw", [128, 48])
        onesb = sb("onesb", [128, 16], BF16)
        zerosb = sb("zerosb", [128, 16], BF16)
        dbuf = sb("dbuf", [128, 4, 128], BF16)

        uo = S2[:].bitcast(BF16)[:, 0:4096].rearrange("p (c n) -> p c n", c=4)
        cact = S2[:].bitcast(BF16)[:, 4096:8192].rearrange("p (c n) -> p c n", c=4)
        mT = S1[:].bitcast(BF16).rearrange("p (c n) -> p c n", c=8)
        slots = big[:].bitcast(BF16)[:, 0:8 * NSLOT * 64].rearrange("p (h s d) -> p h s d", h=8, s=NSLOT)
        kctm = PT[:].rearrange("p a i n -> p (a i n)")[:, 0:1024].rearrange("p (c n) -> p c n", c=2)
        vctm = PT[:].rearrange("p a i n -> p (a i n)")[:, 1024:2048].rearrange("p (c n) -> p c n", c=2)
        lnm = stg
        rt = stg2[:].rearrange("p a n -> p (a n)").rearrange("p (t n) -> p t n", t=8)
        xin = big[:, 0:2048].rearrange("p (b n) -> p b n", b=2)
        ps_all = es.enter_context(nc.psum_tensor("ps", [128, 8, 512], F32))

        def ps(i):
            return ps_all[:, i, :]

        def psb(i):
            return ps_all[:, i, :].bitcast(BF16)

        ident_f = cf[:, 0:128]
        onesD = cf[:, 128:256]
        onesC = cf[:, 256:384]
        flag = cf[:, 384:385]
        epsc = cf[:, 385:386]
        killc = cf[:, 386:387]
        ident_b = cbb[:, 0:128]
        ones_row = crow[0:1, 0:128]
        kill_row = crow[0:1, 128:256]

        pscnt = [0]

        reserved = set()

        def nextps():
            pscnt[0] = (pscnt[0] + 1) % 8
            while pscnt[0] in reserved:
                pscnt[0] = (pscnt[0] + 1) % 8
            return pscnt[0]

        wcnt = [0]

        def wload(src_aps, views=None):
            s = wcnt[0] % NW
            wcnt[0] += 1
            for dst_fn, src in src_aps:
                P.dma("pool", "w%d" % s, dst_fn(wpool[:, s, :]), src, w=[("w", s)])
            return s

        def wview(s, pat, **kw):
            return wpool[:, s, :].rearrange(pat, **kw)

        def dump(name, ap, keys):
            if name in dbg_d and not (os.environ.get("KD_NODUMP") and name != "hT"):
                P.dma("pool", "dbg", dbg_d[name], ap, r=keys)

        P.dma("sp", "c0", cf[:], cf_d, w=["cf"])
        P.dma("pool", "c1", cbb[:], cb_d, w=["cbb"])
        P.dma("pool", "c1", crow[:], crow_d, w=["crow"])
        P.dma("sp", "c0", rope[:].rearrange("p a t d -> p (a t d)"), rope_d, w=["rope"])
        P.dma("sp", "c0", cvs[:], cvec_d, w=["cvs"])
        P.op("act", lambda e: e.activation(out=scol[:], in_=cvs[:], func=AF.Silu), r=["cvs"], w=["scol"])
        P.op("dve", lambda e: e.memset(S2[:], 0.0), w=["S2"])
        P.op("dve", lambda e: e.memset(S3[:], 1.0), w=["S3"])
        P.op("dve", lambda e: e.memset(onesb[:], 1.0), w=["onesb"])
        P.op("dve", lambda e: e.memset(zerosb[:], 0.0), w=["zerosb"])
        P.op("dve", lambda e: e.memset(vca[:].rearrange("p a h d -> p (a h d)"), 1.0), w=["vca"])

        for t in range(8):
            b = t % 2
            P.dma("sp", "xin%d" % b, xin[:, b, :], x_d[t * 128:(t + 1) * 128, :], w=[("xin", b)])
            for cg in range(2):
                pi = nextps()
                for c4 in range(4):
                    c = cg * 4 + c4
                    P.op("pe", lambda e, c=c, c4=c4, pi=pi, b=b: e.transpose(
                        out=ps(pi)[:, c4 * 128:(c4 + 1) * 128], in_=xin[:, b, c * 128:(c + 1) * 128],
                        identity=ident_f), r=[("xin", b), "cf"], w=[("ps", pi)])
                P.op("dve" if cg == 0 else "act", lambda e, cg=cg, pi=pi, t=t: e.tensor_scalar_mul(
                    out=xT[:, cg * 4:(cg + 1) * 4, t * 128:(t + 1) * 128],
                    in0=ps(pi).rearrange("p (c n) -> p c n", c=4), scalar1=ALPHA) if cg == 0 else e.activation(
                    out=xT[:, cg * 4:(cg + 1) * 4, t * 128:(t + 1) * 128],
                    in_=ps(pi).rearrange("p (c n) -> p c n", c=4), func=AF.Copy, scale=ALPHA),
                    r=[("ps", pi)], w=[("xT", c) for c in range(cg * 4, cg * 4 + 4)])

        def chk(k):
            if stop is not None and k >= stop:
                raise _Stop()

        XK = [("xT", c) for c in range(8)]
        HK = [("hT", c) for c in range(8)]

        ada = {}

        def adaln_begin():
            ada["pi"] = nextps()
            reserved.add(ada["pi"])

        def adaln_pieces(L, pieces):
            pi = ada["pi"]
            for piece in pieces:
                src = w_ada_d[L].rearrange("(kc p) n -> p kc n", p=128)[:, :, piece * 512:(piece + 1) * 512]
                s = wload([(lambda v: v.rearrange("p (kc n) -> p kc n", kc=8), src)])
                wv = wview(s, "p (kc n) -> p kc n", kc=8)
                for j in range(4):
                    col = piece * 4 + j
                    for kc in range(8):
                        P.op("pe", lambda e, wv=wv, j=j, kc=kc, col=col, pi=pi: e.matmul(
                            out=ps(pi)[:, col:col + 1], lhsT=wv[:, kc, j * 128:(j + 1) * 128],
                            rhs=scol[:, kc:kc + 1], start=(kc == 0), stop=(kc == 7)),
                            r=[("w", s), "scol"], w=[("ps", pi)])

        def adaln_end_raw():
            pi = ada["pi"]
            P.op("dve", lambda e: e.tensor_copy(out=modraw[:], in_=ps(pi)[:, 0:48]),
                 r=[("ps", pi)], w=["modraw"])
            reserved.discard(pi)

        def adaln_finish(L):
            P.dma("sp", "pv", pv[:], pv_d[L], w=["pv"])
            P.op("dve", lambda e: e.tensor_tensor(out=modv[:, 0:48], in0=modraw[:], in1=pv[:, 0:48],
                                                  op=ALU.add), r=["modraw", "pv"], w=["modv"])
            P.op("dve", lambda e: e.tensor_scalar_add(out=modv[:, 8:16], in0=modv[:, 8:16], scalar1=1.0),
                 r=["modv"], w=["modv"])
            P.op("dve", lambda e: e.tensor_scalar_add(out=modv[:, 32:40], in0=modv[:, 32:40], scalar1=1.0),
                 r=["modv"], w=["modv"])
            P.op("dve", lambda e: e.tensor_tensor(out=modv[:, 48:56], in0=pv[:, 48:56], in1=modv[:, 32:40],
                                                  op=ALU.mult), r=["modv", "pv"], w=["modv"])
            P.op("dve", lambda e: e.tensor_tensor(out=modv[:, 56:64], in0=pv[:, 56:64], in1=modv[:, 32:40],
                                                  op=ALU.mult), r=["modv", "pv"], w=["modv"])
            P.op("dve", lambda e: e.tensor_tensor(out=modv[:, 56:64], in0=modv[:, 56:64], in1=modv[:, 24:32],
                                                  op=ALU.add), r=["modv"], w=["modv"])
            P.op("dve", lambda e: e.tensor_scalar_mul(out=modv[:, 64:72], in0=modv[:, 8:16], scalar1=1.0 / ALPHA),
                 r=["modv"], w=["modv"])
            P.op("dve", lambda e: e.tensor_scalar_mul(out=modv[:, 72:88], in0=pv[:, 48:64], scalar1=ALPHA),
                 r=["modv", "pv"], w=["modv"])
            P.op("dve", lambda e: e.tensor_scalar_mul(out=modv[:, 88:104], in0=pv[:, 64:80], scalar1=ALPHA),
                 r=["modv", "pv"], w=["modv"])

        SH1, SC1P, G1, SH2, SC2P, G2, GS, BS2 = 0, 8, 16, 24, 32, 40, 48, 56
        SC1A, AG1, AB1, AG2, AB2 = 64, 72, 80, 88, 96
        LN1G, LN1B, LN2G, LN2B, CVB, CLG, CLB, CDW = 48, 56, 64, 72, 80, 84, 88, 92

        def ln_fm(src_fn, keys, nch, ones_ap, consume):
            sq = S1[:].rearrange("p (c n) -> p c n", c=8)
            P.alias(S1KEYS, [("sq", c) for c in range(8)])
            for half in range(2):
                pm, pe_ = nextps(), nextps()
                for c in range(nch):
                    P.op("act", lambda e, c=c, half=half: e.activation(
                        out=sq[:, c, :], in_=src_fn(c, half), func=AF.Square),
                        r=[keys[c]], w=[("sq", c)])
                for c in range(nch):
                    P.op("pe", lambda e, c=c, half=half, pm=pm: e.matmul(
                        out=ps(pm), lhsT=ones_ap, rhs=src_fn(c, half), start=(c == 0), stop=(c == nch - 1)),
                        r=[keys[c], "cf"], w=[("ps", pm)])
                for c in range(nch):
                    P.op("pe", lambda e, c=c, pe_=pe_: e.matmul(
                        out=ps(pe_), lhsT=ones_ap, rhs=sq[:, c, :], start=(c == 0), stop=(c == nch - 1)),
                        r=[("sq", c), "cf"], w=[("ps", pe_)])
                mean, rstd, tmp = lnm[:, 0, :], lnm[:, 1, :], lnm[:, 2, :]
                P.op("act", lambda e, pm=pm: e.copy(out=mean, in_=ps(pm)), r=[("ps", pm)], w=[("stg", 0)])
                P.op("dve", lambda e: e.tensor_tensor(out=tmp, in0=mean, in1=mean, op=ALU.mult),
                     r=[("stg", 0)], w=[("stg", 2)])
                P.op("dve", lambda e, pe_=pe_: e.tensor_tensor(out=tmp, in0=ps(pe_), in1=tmp, op=ALU.subtract),
                     r=[("ps", pe_), ("stg", 2)], w=[("stg", 2)])
                P.op("act", lambda e: e.activation(out=rstd, in_=tmp, func=AF.Sqrt, bias=epsc),
                     r=[("stg", 2), "cf"], w=[("stg", 1)])
                P.op("dve", lambda e: e.reciprocal(out=rstd, in_=rstd), r=[("stg", 1)], w=[("stg", 1)])
                for c in range(nch):
                    db = c % 2
                    dap = stg2[:, db, :]
                    P.op("dve", lambda e, c=c, half=half, dap=dap: e.tensor_tensor(
                        out=dap, in0=src_fn(c, half), in1=mean, op=ALU.subtract),
                        r=[keys[c], ("stg", 0)], w=[("stg2", db)])
                    P.op("dve", lambda e, dap=dap: e.tensor_tensor(out=dap, in0=dap, in1=rstd, op=ALU.mult),
                         r=[("stg", 1), ("stg2", db)], w=[("stg2", db)])
                    consume(c, half, dap, ("stg2", db))

        S1KEYS = ([("q_r", t) for t in range(8)] + [("k_r", t) for t in range(8)] + [("u", c) for c in range(4)]
                  + [("sig", c) for c in range(4)] + [("convo", c) for c in range(4)] + [("sq", c) for c in range(8)]
                  + [("mT", c) for c in range(8)])
        S2KEYS = ([("qT", t) for t in range(8)] + [("kT", t) for t in range(8)] + [("ypad", c) for c in range(4)]
                  + [("uo", c) for c in range(4)] + [("cact", c) for c in range(4)])
        PTKEYS = [("PT", a, b) for a in range(2) for b in range(2)] + ["kctm", "vctm"]
        S3KEYS = [("v_aug", t) for t in range(8)] + [("vln", t) for t in range(8)]
        BIGKEYS = ([("bigsq", c) for c in range(4)] + [("merged", c) for c in range(8)]
                   + [("gg", e) for e in range(16)] + ["slots", ("xin", 0), ("xin", 1)])

        def hs(half):
            return slice(half * 512, (half + 1) * 512)

        try:
          chk(1)
          adaln_begin()
          adaln_pieces(0, range(12))
          adaln_end_raw()
          adaln_finish(0)
          chk(2)
          for L in range(nlayers):
              for c in range(8):
                  P.op("act", lambda e, c=c: e.activation(
                      out=hT[:, c, :], in_=xT[:, c, :], func=AF.Identity,
                      scale=modv[:, SC1A + c:SC1A + c + 1], bias=modv[:, SH1 + c:SH1 + c + 1]),
                      r=[("xT", c), "modv"], w=[("hT", c)])
              if L == 0:
                  dump("hT", hT[:, 0, :], HK)

              chk(4)
              q_r = S1[:].bitcast(BF16)[:, 0:4096].rearrange("p (t n) -> p t n", t=8)
              k_r = S1[:].bitcast(BF16)[:, 4096:8192].rearrange("p (t n) -> p t n", t=8)
              qT = S2[:].bitcast(BF16)[:, 0:4096].rearrange("p (c n) -> p c n", c=4)
              kT = S2[:].bitcast(BF16)[:, 4096:8192].rearrange("p (c n) -> p c n", c=4)
              v_aug = S3[:].rearrange("p (t h d) -> p t h d", t=8, h=8)

              def rope_ops(pi, dst, tab, t, rkeys, wkey):
                  src4 = ps(pi).rearrange("p (h b x d) -> p h b x d", h=8, b=2, x=2)
                  A = stg[:, 2, :]
                  B = stg[:, 3, :]
                  A4 = A.rearrange("p (h b x d) -> p h b x d", h=8, b=2, x=2)
                  B4 = B.rearrange("p (h b x d) -> p h b x d", h=8, b=2, x=2)
                  cosv = rope[:, tab, t, :].rearrange("p (b x d) -> p b x d", b=2, x=2)
                  sinv = rope[:, tab + 1, t, :].rearrange("p (b x d) -> p b x d", b=2, x=2)
                  if os.environ.get("KD_NOROPE"):
                      P.op("dve", lambda e: e.tensor_copy(out=dst, in_=ps(pi)), r=rkeys, w=[wkey])
                      return
                  for x in range(2):
                      for bb in range(2):
                          P.op("dve", lambda e, x=x, bb=bb: e.tensor_tensor(
                              out=A4[:, :, bb, x, :], in0=src4[:, :, bb, x, :],
                              in1=cosv[:, bb, x, :].unsqueeze(1).to_broadcast([128, 8, 16]), op=ALU.mult),
                              r=rkeys + ["rope"], w=["ropeA", ("psr", pi)])
                          P.op("dve", lambda e, x=x, bb=bb: e.tensor_tensor(
                              out=B4[:, :, bb, x, :], in0=src4[:, :, bb, 1 - x, :],
                              in1=sinv[:, bb, x, :].unsqueeze(1).to_broadcast([128, 8, 16]), op=ALU.mult),
                              r=rkeys + ["rope"], w=["ropeB", ("psr", pi)])
                  P.op("dve", lambda e: e.tensor_tensor(out=dst, in0=A, in1=B, op=ALU.add),
                       r=["ropeA", "ropeB"], w=[wkey])

              P.alias(S1KEYS, [("q_r", t) for t in range(8)] + [("k_r", t) for t in range(8)])
              P.alias(S2KEYS, [("qT", t) for t in range(8)] + [("kT", t) for t in range(8)])
              P.alias(S3KEYS, [("v_aug", t) for t in range(8)])
              for t in range(0 if not os.environ.get("KD_NOONES") else 0, 8 if not os.environ.get("KD_NOONES") else 0):
                  P.op("dve", lambda e, t=t: e.tensor_copy(out=v_aug[:, t, :, 64:65],
                                                             in_=onesb[:, 0:8].unsqueeze(2)),
                       r=["onesb"], w=[("v_aug", t)])
              for which, c0 in (("k", C_K), ("q", C_Q), ("v", C_V)):
                  if os.environ.get("KD_ONLY") and which not in os.environ.get("KD_ONLY"):
                      continue
                  src = w_in_d[L].rearrange("(kc p) n -> p kc n", p=128)[:, :, c0:c0 + 512]
                  s = wload([(lambda v: v.rearrange("p (kc n) -> p kc n", kc=8), src)])
                  wv = wview(s, "p (kc n) -> p kc n", kc=8)
                  for t in range(8):
                      pi = nextps()
                      for kc in range(8):
                          P.op("pe", lambda e, kc=kc, t=t, pi=pi, wv=wv: e.matmul(
                              out=ps(pi), lhsT=hT[:, kc, t * 128:(t + 1) * 128], rhs=wv[:, kc, :],
                              start=(kc == 0), stop=(kc == 7)), r=[("hT", kc), ("w", s)], w=[("ps", pi)])
                      if which in ("k", "v"):
                          sbi = t % 2
                          P.op("act", lambda e, pi=pi, sbi=sbi: e.copy(out=stg[:, sbi, :], in_=ps(pi)),
                               r=[("ps", pi)], w=[("stg", sbi), ("psr", pi)])
                          dst = (nk_d if which == "k" else nv_d)[L, t * 128:(t + 1) * 128, :]
                          if not os.environ.get("KD_NOSTORE"):
                              P.dma("sp", "o%d" % sbi, dst, stg[:, sbi, :], r=[("stg", sbi)])
                      if which == "k":
                          rope_ops(pi, k_r[:, t, :], 0, t, [("ps", pi)], ("k_r", t))
                      elif which == "q":
                          rope_ops(pi, q_r[:, t, :], 0, t, [("ps", pi)], ("q_r", t))
                      else:
                          P.op("dve", lambda e, pi=pi, t=t: e.tensor_copy(
                              out=v_aug[:, t, :, 0:64], in_=ps(pi).rearrange("p (h d) -> p h d", h=8)),
                              r=[("ps", pi)], w=[("v_aug", t), ("psr", pi)])
              for which in ("k", "q"):
                  if True:
                      srcr, dstT, nm = (k_r, kT, "kT") if which == "k" else (q_r, qT, "qT")
                      for t in range(8):
                          pi = nextps()
                          for c in range(4):
                              P.op("pe", lambda e, c=c, t=t, pi=pi, srcr=srcr: e.transpose(
                                  out=psb(pi)[:, c * 128:(c + 1) * 128], in_=srcr[:, t, c * 128:(c + 1) * 128],
                                  identity=ident_b), r=[(which + "_r", t), "cbb"], w=[("ps", pi)])
                          P.op("act", lambda e, t=t, pi=pi, dstT=dstT: e.activation(
                              out=dstT[:, :, t * 128:(t + 1) * 128],
                              in_=psb(pi)[:, 0:512].rearrange("p (c n) -> p c n", c=4), func=AF.Copy,
                              scale=(0.125 if which == "q" else 1.0)),
                              r=[("ps", pi)], w=[(nm, t)])
              dump("kT", kT[:, 0, :], [("kT", t) for t in range(8)])
              dump("qT", qT[:, 0, :], [("qT", t) for t in range(8)])

              P.alias(BIGKEYS, ["slots"])
              P.alias(PTKEYS, ["kctm", "vctm"])
              P.dma("pool", "sl", slots[:].rearrange("p h s d -> p (h s d)"), slots_d[L], w=["slots"])
              P.dma("pool", "rw", rwb[:].rearrange("p k n -> p (k n)"), rw_d[L], w=["rwb"])
              P.dma("sp", "rb", rbb[:], rb_d[L], w=["rbb"])
              P.dma("pool", "gw", gmw[:].rearrange("p g n -> p (g n)"), gmw_d[L], w=["gmw"])
              P.dma("sp", "gb", gmbs[:].rearrange("p g n -> p (g n)"), gmbs_d[L], w=["gmbs"])
              P.dma("sp", "gl", gmln[:].rearrange("p a n -> p (a n)"), gmln_d[L], w=["gmln"])
              P.dma("pool", "ck", kctm[:], ctxk_d[L].rearrange("(c p) n -> p c n", p=128), w=["kctm"])
              P.dma("pool", "cv", vctm[:], ctxv_d[L].rearrange("(c p) n -> p c n", p=128), w=["vctm"])

              chk(3)
              P.op("dve", lambda e: e.tensor_copy(out=vca[:, :, :, 0:64],
                                                  in_=vctm[:].rearrange("p a (h d) -> p a h d", h=8)),
                   r=["vctm"], w=["vca"])
              for kc2 in range(2):
                  pi = nextps()
                  for c in range(4):
                      P.op("pe", lambda e, c=c, kc2=kc2, pi=pi: e.transpose(
                          out=psb(pi)[:, c * 128:(c + 1) * 128], in_=kctm[:, kc2, c * 128:(c + 1) * 128],
                          identity=ident_b), r=["kctm", "cbb"], w=[("ps", pi)])
                  P.op("act", lambda e, kc2=kc2, pi=pi: e.copy(
                      out=kcT[:, :, kc2 * 128:(kc2 + 1) * 128],
                      in_=psb(pi)[:, 0:512].rearrange("p (c n) -> p c n", c=4)), r=[("ps", pi)], w=["kcT"])

              chk(5)
              P.alias(PTKEYS, [("PT", a, b) for a in range(2) for b in range(2)])
              for j in range(8):
                  lm = _local_chunks(j)
                  vset = (2 * (j // 2), 2 * (j // 2) + 1)
                  tiles = ([("l", m) for m in lm if m in vset] + [("l", m) for m in lm if m not in vset]
                           + [("c", 0), ("c", 1)])
                  assert [t_[1] for t_ in tiles[:2]] == list(vset)
                  pvb = [0, 1]
                  def emit_scores(h):
                      hp, hc = h % 2, h // 2
                      prow = slice(64 * hp, 64 * hp + 64)
                      sbk = [2 + 2 * (h % 2), 3 + 2 * (h % 2)]
                      pb = h % 2
                      for i, (kind, m) in enumerate(tiles):
                          bank = sbk[i // 4]
                          cs = (i % 4) * 128
                          kill = (kind == "c") or (m not in (2 * (j // 2), 2 * (j // 2) + 1))
                          if kind == "l":
                              lhs = kT[prow, hc, m * 128:(m + 1) * 128]
                              rk = [("kT", m)]
                          else:
                              lhs = kcT[prow, hc, m * 128:(m + 1) * 128]
                              rk = ["kcT"]
                          P.op("pe", lambda e, lhs=lhs, bank=bank, cs=cs, kind=kind: e.matmul(
                              out=ps(bank)[:, cs:cs + 128], lhsT=lhs, rhs=qT[prow, hc, j * 128:(j + 1) * 128],
                              start=True, stop=(kind == "c")), r=rk + [("qT", j)], w=[("ps", bank)])
                          if kind == "l":
                              sl0, sl1 = _slot_index(j, m, 0), _slot_index(j, m, 1)
                              if sl1 == sl0 + 1:
                                  P.op("pe", lambda e, bank=bank, cs=cs, sl0=sl0: e.matmul(
                                      out=ps(bank)[:, cs:cs + 128], lhsT=ident_b,
                                      rhs=slots[:, h, sl0:sl0 + 2, :].rearrange("p a d -> p (a d)"),
                                      start=False, stop=True), r=["slots", "cbb"], w=[("ps", bank)])
                              else:
                                  for b in range(2):
                                      sl = (sl0, sl1)[b]
                                      last = (b == 1)
                                      P.op("pe", lambda e, bank=bank, cs=cs, b=b, sl=sl, last=last: e.matmul(
                                          out=ps(bank)[:, cs + b * 64:cs + b * 64 + 64], lhsT=ident_b,
                                          rhs=slots[:, h, sl, :], start=False, stop=last),
                                          r=["slots", "cbb"], w=[("ps", bank)])
                      nt_ = len(tiles)
                      P.op("act", lambda e, pb=pb, sbk=sbk: e.activation(
                          out=PT[:, pb, 0:2, :], in_=ps(sbk[0])[:, 0:256].rearrange("p (i n) -> p i n", i=2),
                          func=AF.Exp), r=[("ps", sbk[0])], w=[("PT", pb, 0)])
                      P.op("act", lambda e, pb=pb, sbk=sbk: e.activation(
                          out=PT[:, pb, 2:4, :], in_=ps(sbk[0])[:, 256:512].rearrange("p (i n) -> p i n", i=2),
                          func=AF.Exp, bias=killc), r=[("ps", sbk[0]), "cf"], w=[("PT", pb, 0)])
                      nb_ = nt_ - 4
                      P.op("act", lambda e, pb=pb, sbk=sbk, nb_=nb_: e.activation(
                          out=PT[:, pb, 4:4 + nb_, :],
                          in_=ps(sbk[1])[:, 0:nb_ * 128].rearrange("p (i n) -> p i n", i=nb_),
                          func=AF.Exp, bias=killc), r=[("ps", sbk[1]), "cf"], w=[("PT", pb, 1)])
                  def emit_pv(h):
                      pb = h % 2
                      nt_ = len(tiles)
                      ob = pvb[h // 4]
                      oc = (h % 4) * 65
                      for i, (kind, m) in enumerate(tiles):
                          if kind == "l":
                              rhs = v_aug[:, m, h, :]
                              rk = [("v_aug", m)]
                          else:
                              rhs = vca[:, m, h, :]
                              rk = ["vca"]
                          P.op("pe", lambda e, i=i, rhs=rhs, ob=ob, oc=oc, pb=pb: e.matmul(
                              out=ps(ob)[:, oc:oc + 65], lhsT=PT[:, pb, i, :], rhs=rhs,
                              start=(i == 0), stop=(i == nt_ - 1)),
                              r=rk + [("PT", pb, i // 4)], w=[("ps", ob)])
                  emit_scores(0)
                  for h in range(8):
                      if h + 1 < 8:
                          emit_scores(h + 1)
                      emit_pv(h)
                  ab = j % 2
                  for hb in range(2):
                      o3 = ps(pvb[hb])[:, 0:260].rearrange("p (h d) -> p h d", h=4)
                      P.op("dve", lambda e, o3=o3, hb=hb, ab=ab: e.reciprocal(
                          out=rcp[:, ab, hb * 4:hb * 4 + 4], in_=o3[:, :, 64]),
                          r=[("ps", pvb[hb])], w=[("rcp", ab, hb)])
                      P.op("dve", lambda e, o3=o3, hb=hb, ab=ab: e.tensor_tensor(
                          out=atm[:, ab, hb * 256:(hb + 1) * 256].rearrange("p (h d) -> p h d", h=4),
                          in0=o3[:, :, 0:64],
                          in1=rcp[:, ab, hb * 4:hb * 4 + 4].unsqueeze(2).to_broadcast([128, 4, 64]),
                          op=ALU.mult), r=[("ps", pvb[hb]), ("rcp", ab, hb)], w=[("atm", ab)])
                  pi = 6 + (j % 2)
                  for c in range(4):
                      P.op("pe", lambda e, c=c, pi=pi, ab=ab: e.transpose(
                          out=psb(pi)[:, c * 128:(c + 1) * 128], in_=atm[:, ab, c * 128:(c + 1) * 128],
                          identity=ident_b), r=[("atm", ab), "cbb"], w=[("ps", pi)])
                  P.op("act", lambda e, j=j, pi=pi: e.copy(
                      out=attT[:, :, j * 128:(j + 1) * 128],
                      in_=psb(pi)[:, 0:512].rearrange("p (c n) -> p c n", c=4)),
                      r=[("ps", pi)], w=[("attT", j)])
              ATK = [("attT", j) for j in range(8)]
              dump("attT", attT[:, 0, :], ATK)

              chk(6)
              sig = S1[:].rearrange("p (c n) -> p c n", c=4)
              ypad = S2[:].bitcast(BF16)[:, 0:4 * 4 * 286].rearrange("p (c s n) -> p c s n", c=4, s=4)
              P.alias(S1KEYS, [("sig", c) for c in range(4)])
              src = w_in_d[L].rearrange("(kc p) n -> p kc n", p=128)[:, :, C_CB:C_CB + 512]
              s = wload([(lambda v: v.rearrange("p (kc n) -> p kc n", kc=8), src)])
              wv = wview(s, "p (kc n) -> p kc n", kc=8)
              for jj in range(4):
                  for half in range(2):
                      pi = nextps()
                      for kc in range(8):
                          P.op("pe", lambda e, kc=kc, jj=jj, half=half, pi=pi, wv=wv: e.matmul(
                              out=ps(pi), lhsT=wv[:, kc, jj * 128:(jj + 1) * 128], rhs=hT[:, kc, hs(half)],
                              start=(kc == 0), stop=(kc == 7)), r=[("hT", kc), ("w", s)], w=[("ps", pi)])
                      P.op("act", lambda e, jj=jj, half=half, pi=pi: e.activation(
                          out=sig[:, jj, hs(half)], in_=ps(pi), func=AF.Sigmoid),
                          r=[("ps", pi)], w=[("sig", jj)])
              src = w_in_d[L].rearrange("(kc p) n -> p kc n", p=128)[:, :, C_CA:C_CA + 512]
              s = wload([(lambda v: v.rearrange("p (kc n) -> p kc n", kc=8), src)])
              wv = wview(s, "p (kc n) -> p kc n", kc=8)
              P.alias(S2KEYS, [("ypad", c) for c in range(4)])
              for jj in range(4):
                  P.op("dve", lambda e, jj=jj: e.tensor_copy(out=ypad[:, jj, 0, 0:15], in_=zerosb[:, 0:15]),
                       r=["zerosb"], w=[("ypad", jj)])
                  P.op("dve", lambda e, jj=jj: e.tensor_copy(out=ypad[:, jj, 3, 271:286], in_=zerosb[:, 0:15]),
                       r=["zerosb"], w=[("ypad", jj)])
              for jj in range(4):
                  for half in range(2):
                      pi = nextps()
                      for kc in range(8):
                          P.op("pe", lambda e, kc=kc, jj=jj, half=half, pi=pi, wv=wv: e.matmul(
                              out=ps(pi), lhsT=wv[:, kc, jj * 128:(jj + 1) * 128], rhs=hT[:, kc, hs(half)],
                              start=(kc == 0), stop=(kc == 7)), r=[("hT", kc), ("w", s)], w=[("ps", pi)])
                      P.op("dve", lambda e, jj=jj, half=half, pi=pi: e.tensor_tensor(
                          out=ypad[:, jj, 2 * half:2 * half + 2, 15:271],
                          in0=ps(pi).rearrange("p (s n) -> p s n", s=2),
                          in1=sig[:, jj, hs(half)].rearrange("p (s n) -> p s n", s=2), op=ALU.mult),
                          r=[("ps", pi), ("sig", jj)], w=[("ypad", jj)])
              convo = S1[:].rearrange("p (c n) -> p c n", c=4)
              P.alias(S1KEYS, [("convo", c) for c in range(4)])
              for jj in range(4):
                  P.op("dve", lambda e, jj=jj: e.tensor_scalar_mul(
                      out=ypad[:, jj, 1:4, 0:15], in0=ypad[:, jj, 0:3, 256:271], scalar1=flag),
                      r=[("ypad", jj), "cf"], w=[("ypad", jj)])
                  P.op("dve", lambda e, jj=jj: e.tensor_scalar_mul(
                      out=ypad[:, jj, 0:3, 271:286], in0=ypad[:, jj, 1:4, 15:30], scalar1=flag),
                      r=[("ypad", jj), "cf"], w=[("ypad", jj)])
              dcnt = 0
              for jj in range(4):
                  pbk = [nextps(), nextps()]
                  for k in range(31):
                      di = dcnt % 4
                      dcnt += 1
                      wk = pv[:, CDW + jj * 31 + k:CDW + jj * 31 + k + 1]
                      P.op("act", lambda e, di=di, wk=wk: e.activation(
                          out=dbuf[:, di, :], in_=ident_b, func=AF.Copy, scale=wk),
                          r=["cbb", "pv"], w=[("dbuf", di)])
                      for half in range(2):
                          P.op("pe", lambda e, jj=jj, k=k, half=half, di=di, pbk=pbk: e.matmul(
                              out=ps(pbk[half]).rearrange("p (s n) -> p s n", s=2), lhsT=dbuf[:, di, :],
                              rhs=ypad[:, jj, 2 * half:2 * half + 2, k:k + 256],
                              start=(k == 0), stop=(k == 30)),
                              r=[("dbuf", di), ("ypad", jj)], w=[("ps", pbk[half])])
                  for half in range(2):
                      P.op("act", lambda e, jj=jj, half=half, pbk=pbk: e.activation(
                          out=convo[:, jj, hs(half)], in_=ps(pbk[half]), func=AF.Identity,
                          bias=pv[:, CVB + jj:CVB + jj + 1]), r=[("ps", pbk[half]), "pv"], w=[("convo", jj)])
              CVK = [("convo", jj) for jj in range(4)]
              dump("convo", convo[:, 0, :], CVK)

              def conv_ln():
                  sqb = big[:].rearrange("p (c n) -> p c n", c=16)
                  P.alias(BIGKEYS, [("bigsq", c) for c in range(4)])
                  for half in range(2):
                      pm, pe_ = nextps(), nextps()
                      for c in range(4):
                          P.op("act", lambda e, c=c, half=half: e.activation(
                              out=sqb[:, c, :], in_=convo[:, c, hs(half)], func=AF.Square),
                              r=[("convo", c)], w=[("bigsq", c)])
                      for c in range(4):
                          P.op("pe", lambda e, c=c, half=half, pm=pm: e.matmul(
                              out=ps(pm), lhsT=onesC, rhs=convo[:, c, hs(half)], start=(c == 0), stop=(c == 3)),
                              r=[("convo", c), "cf"], w=[("ps", pm)])
                      for c in range(4):
                          P.op("pe", lambda e, c=c, pe_=pe_: e.matmul(
                              out=ps(pe_), lhsT=onesC, rhs=sqb[:, c, :], start=(c == 0), stop=(c == 3)),
                              r=[("bigsq", c), "cf"], w=[("ps", pe_)])
                      mean, rstd, tmp = lnm[:, 0, :], lnm[:, 1, :], lnm[:, 2, :]
                      P.op("act", lambda e, pm=pm: e.copy(out=mean, in_=ps(pm)), r=[("ps", pm)], w=[("stg", 0)])
                      P.op("dve", lambda e: e.tensor_tensor(out=tmp, in0=mean, in1=mean, op=ALU.mult),
                           r=[("stg", 0)], w=[("stg", 2)])
                      P.op("dve", lambda e, pe_=pe_: e.tensor_tensor(out=tmp, in0=ps(pe_), in1=tmp,
                                                                    op=ALU.subtract),
                           r=[("ps", pe_), ("stg", 2)], w=[("stg", 2)])
                      P.op("act", lambda e: e.activation(out=rstd, in_=tmp, func=AF.Sqrt, bias=epsc),
                           r=[("stg", 2), "cf"], w=[("stg", 1)])
                      P.op("dve", lambda e: e.reciprocal(out=rstd, in_=rstd), r=[("stg", 1)], w=[("stg", 1)])
                      for c in range(4):
                          db = c % 2
                          dap = stg2[:, db, :]
                          P.op("dve", lambda e, c=c, half=half, dap=dap: e.tensor_tensor(
                              out=dap, in0=convo[:, c, hs(half)], in1=mean, op=ALU.subtract),
                              r=[("convo", c), ("stg", 0)], w=[("stg2", db)])
                          P.op("dve", lambda e, dap=dap: e.tensor_tensor(out=dap, in0=dap, in1=rstd, op=ALU.mult),
                               r=[("stg", 1), ("stg2", db)], w=[("stg2", db)])
                          P.op("act", lambda e, c=c, half=half, dap=dap: e.activation(
                              out=cact[:, c, hs(half)], in_=dap, func=AF.Silu, scale=pv[:, CLG + c:CLG + c + 1],
                              bias=pv[:, CLB + c:CLB + c + 1]), r=[("stg2", db), "pv"], w=[("cact", c)])
              P.alias(S2KEYS, [("cact", c) for c in range(4)])
              conv_ln()
              CAK = [("cact", c) for c in range(4)]
              dump("cact", cact[:, 0, :], CAK)

              chk(7)
              vln = S3[:, 0:4096].rearrange("p (t n) -> p t n", t=8)
              u = S1[:].rearrange("p (c n) -> p c n", c=4)
              P.alias(S3KEYS, [("vln", t) for t in range(8)])

              def gelu_from_ps(pi, dst, dkeys, tmpi):
                  P.op("act", lambda e: e.activation(out=dst, in_=ps(pi), func=AF.Gelu_apprx_tanh),
                       r=[("ps", pi)], w=dkeys)

              src = w_in_d[L].rearrange("(kc p) n -> p kc n", p=128)[:, :, C_GV:C_GV + 512]
              s = wload([(lambda v: v.rearrange("p (kc n) -> p kc n", kc=8), src)])
              wv = wview(s, "p (kc n) -> p kc n", kc=8)
              for t in range(8):
                  pi = nextps()
                  for kc in range(8):
                      P.op("pe", lambda e, kc=kc, t=t, pi=pi, wv=wv: e.matmul(
                          out=ps(pi), lhsT=hT[:, kc, t * 128:(t + 1) * 128], rhs=wv[:, kc, :],
                          start=(kc == 0), stop=(kc == 7)), r=[("hT", kc), ("w", s)], w=[("ps", pi)])
                  gi = t % 2
                  gv_ = stg2[:, gi, :]
                  gelu_from_ps(pi, gv_, [("stg2", gi)], gi)
                  P.op("dve", lambda e, gv_=gv_: e.bn_stats(out=bnst[:, 0:6], in_=gv_),
                       r=[("stg2", gi)], w=["bnst"])
                  P.op("dve", lambda e: e.bn_aggr(out=bnst[:, 6:8], in_=bnst[:, 0:6]), r=["bnst"], w=["bnag"])
                  P.op("act", lambda e: e.activation(out=bnst[:, 7:8], in_=bnst[:, 7:8], func=AF.Sqrt, bias=epsc),
                       r=["bnag", "cf"], w=["bnag"])
                  P.op("dve", lambda e: e.reciprocal(out=bnst[:, 7:8], in_=bnst[:, 7:8]), r=["bnag"], w=["bnag"])
                  P.op("dve", lambda e, gv_=gv_: e.tensor_scalar(
                      out=gv_, in0=gv_, scalar1=bnst[:, 6:7], scalar2=bnst[:, 7:8],
                      op0=ALU.subtract, op1=ALU.mult), r=["bnag", ("stg2", gi)], w=[("stg2", gi)])
                  P.op("dve", lambda e, gv_=gv_: e.tensor_tensor(out=gv_, in0=gv_, in1=gmln[:, 0, :], op=ALU.mult),
                       r=["gmln", ("stg2", gi)], w=[("stg2", gi)])
                  P.op("dve", lambda e, gv_=gv_, t=t: e.tensor_tensor(
                      out=vln[:, t, :], in0=gv_, in1=gmln[:, 1, :], op=ALU.add),
                      r=["gmln", ("stg2", gi)], w=[("vln", t)])

              src = w_in_d[L].rearrange("(kc p) n -> p kc n", p=128)[:, :, C_GU:C_GU + 512]
              s = wload([(lambda v: v.rearrange("p (kc n) -> p kc n", kc=8), src)])
              wv = wview(s, "p (kc n) -> p kc n", kc=8)
              P.alias(S1KEYS, [("u", c) for c in range(4)])
              for jj in range(4):
                  for half in range(2):
                      pi = nextps()
                      for kc in range(8):
                          P.op("pe", lambda e, kc=kc, jj=jj, half=half, pi=pi, wv=wv: e.matmul(
                              out=ps(pi), lhsT=wv[:, kc, jj * 128:(jj + 1) * 128], rhs=hT[:, kc, hs(half)],
                              start=(kc == 0), stop=(kc == 7)), r=[("hT", kc), ("w", s)], w=[("ps", pi)])
                      gelu_from_ps(pi, u[:, jj, hs(half)], [("u", jj)], half)
              P.alias(S2KEYS, [("uo", c) for c in range(4)])
              for g in range(4):
                  for nb in range(2):
                      pi = nextps()
                      for n4 in range(4):
                          n = nb * 4 + n4
                          P.op("pe", lambda e, g=g, n=n, n4=n4, pi=pi: e.matmul(
                              out=ps(pi)[:, n4 * 128:(n4 + 1) * 128], lhsT=vln[:, n, g * 128:(g + 1) * 128],
                              rhs=gmw[:, g, :], start=True, stop=True), r=[("vln", n), "gmw"], w=[("ps", pi)])
                      tb = nb
                      P.op("dve", lambda e, g=g, pi=pi, tb=tb: e.tensor_tensor(
                          out=stg[:, tb, :].rearrange("p (a n) -> p a n", a=4),
                          in0=ps(pi).rearrange("p (a n) -> p a n", a=4),
                          in1=gmbs[:, g, :].unsqueeze(1).to_broadcast([128, 4, 128]), op=ALU.add),
                          r=[("ps", pi), "gmbs"], w=[("stg", tb)])
                      P.op("dve", lambda e, g=g, nb=nb, tb=tb: e.tensor_tensor(
                          out=uo[:, g, nb * 512:(nb + 1) * 512], in0=stg[:, tb, :],
                          in1=u[:, g, nb * 512:(nb + 1) * 512], op=ALU.mult),
                          r=[("stg", tb), ("u", g)], w=[("uo", g)])
              UOK = [("uo", g) for g in range(4)]
              dump("uo", uo[:, 0, :], UOK)

              chk(8)
              merged = big[:].rearrange("p (c n) -> p c n", c=8)
              P.alias(S1KEYS, [("mT", c) for c in range(8)])
              P.alias(BIGKEYS, [("merged", c) for c in range(8)])
              branches = [(1, na_out_d, attT, ATK), (2, gm_out_d, uo, UOK), (0, conv_pw_d, cact, CAK)]
              for bi_, (bidx, wd, actT, akeys) in enumerate(branches):
                  sp_ = wload([(lambda v: v.rearrange("p (cc n) -> p cc n", cc=4),
                                wd[L].rearrange("(cc p) n -> p cc n", p=128))])
                  pw = wview(sp_, "p (cc n) -> p cc n", cc=4)
                  for piece in range(2):
                      c0 = C_GZ + bidx * 1024 + piece * 512
                      src = w_in_d[L].rearrange("(kc p) n -> p kc n", p=128)[:, :, c0:c0 + 512]
                      s = wload([(lambda v: v.rearrange("p (kc n) -> p kc n", kc=8), src)])
                      wv = wview(s, "p (kc n) -> p kc n", kc=8)
                      for jj in range(4):
                          c = piece * 4 + jj
                          for half in range(2):
                              pa, pb_ = nextps(), nextps()
                              for kc in range(8):
                                  P.op("pe", lambda e, kc=kc, jj=jj, half=half, pa=pa, wv=wv: e.matmul(
                                      out=ps(pa), lhsT=wv[:, kc, jj * 128:(jj + 1) * 128], rhs=hT[:, kc, hs(half)],
                                      start=(kc == 0), stop=(kc == 7)), r=[("hT", kc), ("w", s)], w=[("ps", pa)])
                              for cc in range(4):
                                  P.op("pe", lambda e, cc=cc, c=c, half=half, pb_=pb_, pw=pw, actT=actT: e.matmul(
                                      out=ps(pb_), lhsT=pw[:, cc, c * 128:(c + 1) * 128], rhs=actT[:, cc, hs(half)],
                                      start=(cc == 0), stop=(cc == 3)), r=akeys + [("w", sp_)], w=[("ps", pb_)])
                              si = half
                              P.op("act", lambda e, pa=pa, si=si: e.activation(
                                  out=stg[:, si, :], in_=ps(pa), func=AF.Sigmoid),
                                  r=[("ps", pa)], w=[("stg", si)])
                              if bi_ == 0:
                                  P.op("dve", lambda e, c=c, half=half, pb_=pb_, si=si: e.tensor_tensor(
                                      out=merged[:, c, hs(half)], in0=ps(pb_), in1=stg[:, si, :], op=ALU.mult),
                                      r=[("ps", pb_), ("stg", si)], w=[("merged", c)])
                              else:
                                  P.op("dve", lambda e, pb_=pb_, si=si: e.tensor_tensor(
                                      out=stg[:, si, :], in0=ps(pb_), in1=stg[:, si, :], op=ALU.mult),
                                      r=[("ps", pb_), ("stg", si)], w=[("stg", si)])
                                  if bi_ == 1:
                                      P.op("dve", lambda e, c=c, half=half, si=si: e.tensor_tensor(
                                          out=merged[:, c, hs(half)], in0=merged[:, c, hs(half)],
                                          in1=stg[:, si, :], op=ALU.add),
                                          r=[("merged", c), ("stg", si)], w=[("merged", c)])
                                  else:
                                      P.op("dve", lambda e, c=c, half=half, si=si: e.tensor_tensor(
                                          out=mT[:, c, hs(half)], in0=merged[:, c, hs(half)],
                                          in1=stg[:, si, :], op=ALU.add),
                                          r=[("merged", c), ("stg", si)], w=[("mT", c)])
              MTK = [("mT", c) for c in range(8)]
              dump("mT", mT[:, 0, :], MTK)

              chk(9)
              for piece in range(2):
                  src = w_o_d[L].rearrange("(kc p) n -> p kc n", p=128)[:, :, piece * 512:(piece + 1) * 512]
                  s = wload([(lambda v: v.rearrange("p (kc n) -> p kc n", kc=8), src)])
                  wv = wview(s, "p (kc n) -> p kc n", kc=8)
                  for jj in range(4):
                      c = piece * 4 + jj
                      for half in range(2):
                          pi = nextps()
                          for kc in range(8):
                              P.op("pe", lambda e, kc=kc, jj=jj, half=half, pi=pi, wv=wv: e.matmul(
                                  out=ps(pi), lhsT=wv[:, kc, jj * 128:(jj + 1) * 128], rhs=mT[:, kc, hs(half)],
                                  start=(kc == 0), stop=(kc == 7)), r=[("mT", kc), ("w", s)], w=[("ps", pi)])
                          P.op("dve", lambda e, c=c, half=half, pi=pi: e.scalar_tensor_tensor(
                              out=xT[:, c, hs(half)], in0=ps(pi), scalar=modv[:, G1 + c:G1 + c + 1],
                              in1=xT[:, c, hs(half)], op0=ALU.mult, op1=ALU.add),
                              r=[("ps", pi), "modv", ("xT", c)], w=[("xT", c)])
              dump("r1", xT[:, 0, :], XK)

              chk(10)
              def cons1(c, half, dap, dkey):
                  P.op("act", lambda e: e.activation(
                      out=hT[:, c, hs(half)], in_=dap, func=AF.Identity, scale=modv[:, GS + c:GS + c + 1],
                      bias=modv[:, BS2 + c:BS2 + c + 1]), r=[dkey, "modv"], w=[("hT", c)])
                  P.op("act", lambda e: e.activation(
                      out=xT[:, c, hs(half)], in_=dap, func=AF.Identity, scale=modv[:, AG1 + c:AG1 + c + 1],
                      bias=modv[:, AB1 + c:AB1 + c + 1]), r=[dkey, "modv"], w=[("xT", c)])
              ln_fm(lambda c, half: xT[:, c, hs(half)], XK, 8, onesD, cons1)
              dump("tT", hT[:, 0, :], HK)

              chk(11)
              NXT = L + 1 < nlayers
              if NXT:
                  adaln_begin()
                  adaln_pieces(L + 1, range(0, 2))
              lg = rt[:, :, 0:36]
              for t in range(8):
                  pi = nextps()
                  for kc in range(8):
                      P.op("pe", lambda e, kc=kc, t=t, pi=pi: e.matmul(
                          out=ps(pi)[:, 0:36], lhsT=hT[:, kc, t * 128:(t + 1) * 128], rhs=rwb[:, kc, :],
                          start=(kc == 0), stop=(kc == 7)), r=[("hT", kc), "rwb"], w=[("ps", pi)])
                  P.op("dve", lambda e, t=t, pi=pi: e.tensor_tensor(
                      out=rt[:, t, 0:36], in0=ps(pi)[:, 0:36], in1=rbb[:], op=ALU.add),
                      r=[("ps", pi), "rbb"], w=[("stg2", 0), ("stg2", 1)])
              gl = rt[:, :, 0:4]
              el = rt[:, :, 4:36]
              gmax, oneh, gsum, esel = rt[:, :, 36:37], rt[:, :, 40:44], rt[:, :, 37:38], rt[:, :, 44:52]
              prod = rt[:, :, 52:84]
              m1, eq1, e2, m2 = rt[:, :, 38:39], rt[:, :, 84:92], rt[:, :, 92:100], rt[:, :, 39:40]
              msk, ex, den, g8 = rt[:, :, 116:124], rt[:, :, 108:116], rt[:, :, 100:101], rt[:, :, 120:128]
              gex = rt[:, :, 102:106]

              def R(fn):
                  P.op("dve", fn, r=[("stg2", 0), ("stg2", 1)], w=[("stg2", 0), ("stg2", 1)])
              R(lambda e: e.tensor_reduce(out=gmax, in_=gl, axis=AX.X, op=ALU.max))
              R(lambda e: e.tensor_tensor(out=oneh, in0=gl, in1=gmax.to_broadcast([128, 8, 4]), op=ALU.is_equal))
              R(lambda e: e.tensor_tensor(out=gex, in0=gl, in1=gmax.to_broadcast([128, 8, 4]), op=ALU.subtract))
              P.op("act", lambda e: e.activation(out=gex, in_=gex, func=AF.Exp), r=[("stg2", 0), ("stg2", 1)], w=[("stg2", 0), ("stg2", 1)])
              R(lambda e: e.tensor_reduce(out=gsum, in_=gex, axis=AX.X, op=ALU.add))
              R(lambda e: e.tensor_tensor(
                  out=prod.rearrange("p t (g x) -> p t g x", g=4), in0=el.rearrange("p t (g x) -> p t g x", g=4),
                  in1=oneh.unsqueeze(3).to_broadcast([128, 8, 4, 8]), op=ALU.mult))
              R(lambda e: e.tensor_reduce(out=esel, in_=prod.rearrange("p t (g x) -> p t x g", g=4),
                                          axis=AX.X, op=ALU.add))
              R(lambda e: e.tensor_reduce(out=m1, in_=esel, axis=AX.X, op=ALU.max))
              R(lambda e: e.tensor_tensor(out=eq1, in0=esel, in1=m1.to_broadcast([128, 8, 8]), op=ALU.is_equal))
              R(lambda e: e.scalar_tensor_tensor(out=e2, in0=eq1, scalar=NEG, in1=esel, op0=ALU.mult, op1=ALU.add))
              R(lambda e: e.tensor_reduce(out=m2, in_=e2, axis=AX.X, op=ALU.max))
              R(lambda e: e.tensor_tensor(out=msk, in0=esel, in1=m2.to_broadcast([128, 8, 8]), op=ALU.is_ge))
              R(lambda e: e.tensor_tensor(out=ex, in0=esel, in1=m1.to_broadcast([128, 8, 8]), op=ALU.subtract))
              P.op("act", lambda e: e.activation(out=ex, in_=ex, func=AF.Exp), r=[("stg2", 0), ("stg2", 1)], w=[("stg2", 0), ("stg2", 1)])
              R(lambda e: e.tensor_tensor(out=ex, in0=ex, in1=msk, op=ALU.mult))
              R(lambda e: e.tensor_reduce(out=den, in_=ex, axis=AX.X, op=ALU.add))
              R(lambda e: e.tensor_tensor(out=den, in0=den, in1=gsum, op=ALU.mult))
              R(lambda e: e.reciprocal(out=den, in_=den))
              R(lambda e: e.tensor_tensor(out=g8, in0=ex, in1=den.to_broadcast([128, 8, 8]), op=ALU.mult))
              P.op("dve", lambda e: e.tensor_tensor(
                  out=gtm[:].rearrange("p t (g x) -> p t g x", g=4),
                  in0=oneh.unsqueeze(3).to_broadcast([128, 8, 4, 8]),
                  in1=g8.unsqueeze(2).to_broadcast([128, 8, 4, 8]), op=ALU.mult), r=[("stg2", 0), ("stg2", 1)], w=["gtm"])
              dump("gtm", gtm[:].rearrange("p t e -> p (t e)"), ["gtm"])
              if NXT:
                  adaln_pieces(L + 1, range(2, 4))
              for tg in range(2):
                  pi = nextps()
                  for t4 in range(4):
                      t = tg * 4 + t4
                      P.op("pe", lambda e, t=t, t4=t4, pi=pi: e.transpose(
                          out=ps(pi)[0:32, t4 * 128:(t4 + 1) * 128], in_=gtm[:, t, :], identity=ident_f),
                          r=["gtm", "cf"], w=[("ps", pi)])
                  P.op("act", lambda e, tg=tg, pi=pi: e.copy(out=gateT[:, tg * 512:(tg + 1) * 512],
                                                             in_=ps(pi)[0:32, :]),
                       r=[("ps", pi)], w=["gateT"])

              chk(12)
              g_grp = big[:].bitcast(BF16).rearrange("p (e n) -> p e n", e=16)
              P.alias(BIGKEYS, [("gg", e) for e in range(16)])
              for grp in range(2):
                  for ep in range(8):
                      e0 = grp * 16 + ep * 2
                      s = wload([
                          (lambda v: v.rearrange("p (a e kc f) -> p a e kc f", a=2, e=2, kc=8)[:, 0],
                           w1_d[L, e0:e0 + 2].rearrange("e (kc p) f -> p e kc f", p=128)),
                          (lambda v: v.rearrange("p (a e kc f) -> p a e kc f", a=2, e=2, kc=8)[:, 1],
                           w3_d[L, e0:e0 + 2].rearrange("e (kc p) f -> p e kc f", p=128)),
                      ])
                      wv = wview(s, "p (a e kc f) -> p a e kc f", a=2, e=2, kc=8)
                      for ei in range(2):
                          e_glob = e0 + ei
                          e_loc = ep * 2 + ei
                          for half in range(2):
                              p1, p3, pg = nextps(), nextps(), nextps()
                              for kc in range(8):
                                  P.op("pe", lambda e, kc=kc, ei=ei, half=half, p1=p1, wv=wv: e.matmul(
                                      out=ps(p1), lhsT=wv[:, 0, ei, kc, :], rhs=hT[:, kc, hs(half)],
                                      start=(kc == 0), stop=(kc == 7)), r=[("hT", kc), ("w", s)], w=[("ps", p1)])
                              for kc in range(8):
                                  P.op("pe", lambda e, kc=kc, ei=ei, half=half, p3=p3, wv=wv: e.matmul(
                                      out=ps(p3), lhsT=wv[:, 1, ei, kc, :], rhs=hT[:, kc, hs(half)],
                                      start=(kc == 0), stop=(kc == 7)), r=[("hT", kc), ("w", s)], w=[("ps", p3)])
                              P.op("pe", lambda e, e_glob=e_glob, half=half, pg=pg: e.matmul(
                                  out=ps(pg), lhsT=cbb[0:32, 128 + e_glob * 128:128 + (e_glob + 1) * 128],
                                  rhs=gateT[:, hs(half)], start=True, stop=True),
                                  r=["gateT", "cbb"], w=[("ps", pg)])
                              si = half
                              P.op("act", lambda e, p1=p1, si=si: e.activation(
                                  out=stg[:, si, :], in_=ps(p1), func=AF.Silu), r=[("ps", p1)], w=[("stg", si)])
                              P.op("dve", lambda e, p3=p3, si=si: e.tensor_tensor(
                                  out=stg[:, si, :], in0=ps(p3), in1=stg[:, si, :], op=ALU.mult),
                                  r=[("ps", p3), ("stg", si)], w=[("stg", si)])
                              P.op("dve", lambda e, pg=pg, si=si, e_loc=e_loc, half=half: e.tensor_tensor(
                                  out=g_grp[:, e_loc, hs(half)], in0=ps(pg), in1=stg[:, si, :], op=ALU.mult),
                                  r=[("ps", pg), ("stg", si)], w=[("gg", e_loc)])
                      if grp == 0 and NXT:
                          adaln_pieces(L + 1, [4 + ep])
                  if grp == 0 and NXT:
                      adaln_end_raw()
                  for pss in range(2):
                      ss = []
                      for e8 in range(2):
                          e0 = grp * 16 + e8 * 8
                          ss.append(wload([(lambda v: v.rearrange("p (e n) -> p e n", e=8),
                                            w2_d[L, e0:e0 + 8, :, pss * 512:(pss + 1) * 512].rearrange(
                                                "e f n -> f e n"))]))
                      for e_loc in range(16):
                          wv = wview(ss[e_loc // 8], "p (e n) -> p e n", e=8)
                          for dc in range(4):
                              for half in range(2):
                                  bk = dc * 2 + half
                                  P.op("pe", lambda e, e_loc=e_loc, dc=dc, half=half, bk=bk, wv=wv: e.matmul(
                                      out=ps(bk), lhsT=wv[:, e_loc % 8, dc * 128:(dc + 1) * 128],
                                      rhs=g_grp[:, e_loc, hs(half)], start=(e_loc == 0), stop=(e_loc == 15)),
                                      r=[("gg", e_loc), ("w", ss[e_loc // 8])], w=[("ps", bk)])
                      for dc in range(4):
                          c = pss * 4 + dc
                          for half in range(2):
                              bk = dc * 2 + half
                              P.op("dve", lambda e, c=c, half=half, bk=bk: e.scalar_tensor_tensor(
                                  out=xT[:, c, hs(half)], in0=ps(bk), scalar=modv[:, G2 + c:G2 + c + 1],
                                  in1=xT[:, c, hs(half)], op0=ALU.mult, op1=ALU.add),
                                  r=[("ps", bk), "modv", ("xT", c)], w=[("xT", c)])
              dump("r2", xT[:, 0, :], XK)

              chk(13)
              lg_, lb_ = LN2G, LN2B

              def cons2(c, half, dap, dkey):
                  P.op("act", lambda e: e.activation(
                      out=xT[:, c, hs(half)], in_=dap, func=AF.Identity,
                      scale=(modv[:, AG2 + c:AG2 + c + 1] if L + 1 < nlayers else pv[:, lg_ + c:lg_ + c + 1]),
                      bias=(modv[:, AB2 + c:AB2 + c + 1] if L + 1 < nlayers else pv[:, lb_ + c:lb_ + c + 1])),
                      r=[dkey, "pv", "modv"], w=[("xT", c)])
              ln_fm(lambda c, half: xT[:, c, hs(half)], XK, 8, onesD, cons2)
              if L + 1 < nlayers:
                  adaln_finish(L + 1)

        except _Stop:
            pass
        P.alias(BIGKEYS, [("xin", 0), ("xin", 1)])
        for t in range(8):
            b = t % 2
            for cg in range(2):
                pi = nextps()
                for c4 in range(4):
                    c = cg * 4 + c4
                    P.op("pe", lambda e, c=c, c4=c4, pi=pi, t=t: e.transpose(
                        out=ps(pi)[:, c4 * 128:(c4 + 1) * 128], in_=xT[:, c, t * 128:(t + 1) * 128],
                        identity=ident_f), r=[("xT", c), "cf"], w=[("ps", pi)])
                P.op("act", lambda e, cg=cg, pi=pi, b=b: e.copy(out=xin[:, b, cg * 512:(cg + 1) * 512], in_=ps(pi)),
                     r=[("ps", pi)], w=[("xin", b)])
            P.dma("sp", "xo%d" % b, y_d[t * 128:(t + 1) * 128, :], xin[:, b, :], r=[("xin", b)])
        P.finish()
    return nc


def _rope_tables(sample):
    t = np.arange(NT)
    out = np.zeros((128, 2, 8, 64), np.float32)
    if sample:
        pr = (t // 64).astype(np.float32)
        pc = (t % 64).astype(np.float32)
        inv = (10000.0 ** (-np.arange(0, 32, 2, dtype=np.float32) / 32)).astype(np.float32)
        cosv = np.zeros((NT, 64), np.float32)
        sinv = np.zeros((NT, 64), np.float32)
        for pos, off in ((pr, 0), (pc, 32)):
            ang = pos[:, None] * inv[None, :]
            c, s = np.cos(ang), np.sin(ang)
            cosv[:, off:off + 16] = c
            cosv[:, off + 16:off + 32] = c
            sinv[:, off:off + 16] = -s
            sinv[:, off + 16:off + 32] = s
    else:
        cosv = np.ones((NT, 64), np.float32)
        sinv = np.zeros((NT, 64), np.float32)
    cosv = cosv.reshape(8, 128, 64).transpose(1, 0, 2)
    sinv = sinv.reshape(8, 128, 64).transpose(1, 0, 2)
    out[:, 0] = cosv
    out[:, 1] = sinv
    return out.reshape(128, -1)


def _slots_table(rpb):
    L = rpb.shape[0]
    cq = np.arange(64)
    ck = np.arange(64)
    cstart = np.clip(cq - 8, 0, 48)
    valid = (ck[:, None] >= cstart[None, :]) & (ck[:, None] < cstart[None, :] + 16)
    dc = np.clip(ck[:, None] - cq[None, :] + 15, 0, 30)
    T = rpb[:, :, :, dc]
    T = np.where(valid[None, None, None], T, np.float32(NEG)).astype(np.float32)
    negb = np.full((L, 8, 64, 64), NEG, np.float32)
    out = np.zeros((L, 128, 8, NSLOT, 64), np.float32)
    for s in range(14):
        out[:, 0:64, :, 13 - s, :] = T[:, :, s].transpose(0, 2, 1, 3)
        out[:, 64:128, :, 13 - s, :] = T[:, :, s + 1].transpose(0, 2, 1, 3)
    out[:, 0:64, :, 14, :] = negb.transpose(0, 2, 1, 3)
    out[:, 64:128, :, 14, :] = T[:, :, 3].transpose(0, 2, 1, 3)
    out[:, 0:64, :, 15, :] = T[:, :, 10].transpose(0, 2, 1, 3)
    out[:, 64:128, :, 15, :] = negb.transpose(0, 2, 1, 3)
    out[:, :, :, 16, :] = NEG
    return out.reshape(L, 128, -1)


def _prep(inputs):
    f = lambda a: np.ascontiguousarray(np.asarray(a, dtype=np.float32))
    I = {k: f(v) for k, v in inputs.items()}
    L = DEPTH
    pv = np.zeros((L, 128, NPV), np.float32)
    pv[:, :, 0:48] = I["b_ada"].reshape(L, 48, 128).transpose(0, 2, 1)
    for name, off, n in (("ln1_g", 48, 8), ("ln1_b", 56, 8), ("ln2_g", 64, 8), ("ln2_b", 72, 8),
                         ("conv_b", 80, 4), ("conv_ln_g", 84, 4), ("conv_ln_b", 88, 4)):
        pv[:, :, off:off + n] = I[name].reshape(L, n, 128).transpose(0, 2, 1)
    cd = I["conv_dw"].reshape(L, 31, 4, 128)
    pv[:, :, 92:216] = cd.transpose(0, 3, 2, 1).reshape(L, 128, 124)
    rw = np.zeros((L, 128, 8, 36), np.float32)
    rw[:, :, :, 0:4] = I["rg_w"].reshape(L, 8, 128, 4).transpose(0, 2, 1, 3)
    rw[:, :, :, 4:36] = I["re_w"].reshape(L, 4, 8, 128, 8).transpose(0, 3, 2, 1, 4).reshape(L, 128, 8, 32)
    rb = np.concatenate([I["rg_b"], I["re_b"].reshape(L, 32)], axis=1)
    rb = np.ascontiguousarray(np.broadcast_to(rb[:, None, :], (L, 128, 36)))
    gmw = np.ascontiguousarray(I["gm_ws"].transpose(0, 3, 1, 2)).reshape(L, 128, 512)
    gmbs = np.ascontiguousarray(np.broadcast_to(I["gm_bs"].reshape(L, 1, 512), (L, 128, 512)))
    gmln = np.concatenate([I["gm_ln_g"], I["gm_ln_b"]], axis=1)
    gmln = np.ascontiguousarray(np.broadcast_to(gmln[:, None, :], (L, 128, 1024)))
    slots_s = _slots_table(I["na_rpb"])
    slots_p = np.zeros_like(slots_s)
    ident = np.eye(128, dtype=np.float32)
    sel = np.zeros((128, 32 * 128), np.float32)
    for e in range(32):
        sel[e, e * 128:(e + 1) * 128] = 1.0
    cb = np.concatenate([ident, sel], axis=1)

    def cf(sample):
        a = np.zeros((128, 387), np.float32)
        a[:, 0:128] = ident
        a[:, 128:256] = 1.0 / 1024
        a[:, 256:384] = 1.0 / 512
        a[:, 384] = 1.0 if sample else 0.0
        a[:, 385] = LN_EPS
        a[:, 386] = 0.0 if sample else NEG
        return a

    def crow(sample):
        a = np.zeros((1, 256), np.float32)
        a[0, 0:128] = 1.0
        a[0, 128:256] = 0.0 if sample else NEG
        return a
    shared = dict(w_ada=I["w_ada"], w_in=I["w_in"], conv_pw=I["conv_pw"], na_out=I["na_out"],
                  gm_out=I["gm_out"], w_o=I["w_o"], moe_w1=I["moe_w1"], moe_w3=I["moe_w3"],
                  moe_w2=I["moe_w2"], pv=pv, rw=rw.reshape(L, 128, 288), rb=rb, gmw=gmw, gmbs=gmbs,
                  gmln=gmln, cb=cb)
    rope_s, rope_p = _rope_tables(True), _rope_tables(False)
    cf_s, cf_p, crow_s, crow_p = cf(True), cf(False), crow(True), crow(False)
    zero_ctx = np.zeros((L, 256, 512), np.float32)
    maps = []
    for core in range(8):
        m = dict(shared)
        if core in (4, 5):
            b = core - 4
            m["x"] = I["x_sample"][b]
            m["cvec"] = np.ascontiguousarray(I["c"][b].reshape(8, 128).T)
            m["ctxk"] = np.ascontiguousarray(I["cache_na_k"][b].reshape(L, 256, 512))
            m["ctxv"] = np.ascontiguousarray(I["cache_na_v"][b].reshape(L, 256, 512))
            m["slots"], m["rope"], m["cf"], m["crow"] = slots_s, rope_s, cf_s, crow_s
        else:
            i = core if core < 4 else core - 6
            m["x"] = np.ascontiguousarray(I["x_prompt"][4 * i:4 * i + 4].reshape(NT, D))
            m["cvec"] = np.ascontiguousarray(I["c_ctx"].reshape(8, 128).T)
            m["ctxk"], m["ctxv"] = zero_ctx, zero_ctx
            m["slots"], m["rope"], m["cf"], m["crow"] = slots_p, rope_p, cf_p, crow_p
        maps.append(m)
    return maps


_NC_CACHE = {}


def kernel(**inputs):
    maps = _prep(inputs)
    if "nc" not in _NC_CACHE:
        _NC_CACHE["nc"] = build_nc()
    nc = _NC_CACHE["nc"]
    res = run_bass_kernel_spmd(nc, maps, core_ids=list(range(8)))
    R = res.results
    y_prompt = np.stack([R[i]["y"] for i in range(4)], 0).reshape(16, 256, D).astype(np.float32)
    y_sample = np.stack([R[4]["y"], R[5]["y"]], 0).astype(np.float32)
    nk = np.zeros((16, DEPTH, 256, 8, 64), np.float32)
    nv = np.zeros((16, DEPTH, 256, 8, 64), np.float32)
    for i in range(4):
        k = R[i]["newk"].reshape(DEPTH, 4, 256, 8, 64).transpose(1, 0, 2, 3, 4)
        v = R[i]["newv"].reshape(DEPTH, 4, 256, 8, 64).transpose(1, 0, 2, 3, 4)
        nk[4 * i:4 * i + 4] = k
        nv[4 * i:4 * i + 4] = v
    return (y_prompt, y_sample, nk, nv)
```

```python
import os
import numpy as np
import concourse.bass as bass
import concourse.mybir as mybir
from concourse.bass_utils import run_bass_kernel_spmd

F32 = mybir.dt.float32
BF16 = mybir.dt.bfloat16
AF = mybir.ActivationFunctionType
ALU = mybir.AluOpType
AX = mybir.AxisListType

D = 1024
NT = 1024
DEPTH = 4
D_IN = 6656
ALPHA = (2 * DEPTH) ** 0.25
LN_EPS = 1e-5
NEG = -1e30
NPV = 216
NSLOT = 17
GELU_C = 0.7978845608028654

C_CA, C_CB, C_Q, C_K, C_V, C_GU, C_GV, C_GZ = 0, 512, 1024, 1536, 2048, 2560, 3072, 3584


class Prog:
    def __init__(self, nc, sems):
        self.nc = nc
        self.engs = {"pe": nc.tensor, "act": nc.scalar, "dve": nc.vector, "pool": nc.gpsimd, "sp": nc.sync}
        self.sem = {e: sems[i] for i, e in enumerate(self.engs)}
        self.free_sems = list(sems[len(self.engs):])
        self.cnt = {e: 0 for e in self.engs}
        self.seen = {e: {} for e in self.engs}
        self.last_w = {}
        self.readers = {}
        self.dsem = {}

    def _wait(self, eng, tok):
        s, v = tok
        if s == eng and eng == "pe":
            return
        if self.seen[eng].get(s, 0) >= v:
            return
        if s in self.sem:
            h = self.sem[s]
        else:
            h, v = self.dsem[s][0], self.dsem[s][1]
        self.engs[eng].wait_ge(h, v)
        self.seen[eng][s] = v

    def _deps(self, eng, r, w, nowaw=False):
        for k in r:
            t = self.last_w.get(k)
            if t is not None:
                self._wait(eng, t)
        for k in w:
            t = self.last_w.get(k)
            if t is not None and not nowaw:
                self._wait(eng, t)
            for s, v in self.readers.get(k, {}).items():
                self._wait(eng, (s, v))

    def _record(self, tok, r, w):
        s, v = tok
        for k in r:
            d = self.readers.setdefault(k, {})
            d[s] = max(d.get(s, 0), v)
        for k in w:
            self.last_w[k] = tok
            self.readers[k] = {}

    def op(self, eng, fn, r=(), w=()):
        self._deps(eng, r, w)
        ins = fn(self.engs[eng])
        self.cnt[eng] += 1
        ins.then_inc(self.sem[eng], 1)
        self._record((eng, self.cnt[eng]), r, w)

    def dma(self, q, slot, out, in_, r=(), w=()):
        if slot not in self.dsem:
            self.dsem[slot] = [self.free_sems.pop(), 0]
        self._deps(q, r, w, nowaw=True)
        ins = self.engs[q].dma_start(out=out, in_=in_)
        self.dsem[slot][1] += 16
        ins.then_inc(self.dsem[slot][0], 16)
        self._record((slot, self.dsem[slot][1]), r, w)

    def alias(self, old_keys, new_keys):
        acc = {}
        for k in old_keys:
            t = self.last_w.get(k)
            if t is not None:
                acc[t[0]] = max(acc.get(t[0], 0), t[1])
            for s_, v in self.readers.get(k, {}).items():
                acc[s_] = max(acc.get(s_, 0), v)
        for k in new_keys:
            d = self.readers.setdefault(k, {})
            for s_, v in acc.items():
                d[s_] = max(d.get(s_, 0), v)

    def finish(self):
        for slot, (h, v) in self.dsem.items():
            if v:
                self.engs["sp"].wait_ge(h, v)
        for e in self.engs:
            if e != "sp" and self.cnt[e]:
                self.engs["sp"].wait_ge(self.sem[e], self.cnt[e])


def _slot_index(j, m, b):
    r = 2 * j + b
    rs = min(max(r - 4, 0), 8)
    v0 = rs <= 2 * m < rs + 8
    v1 = rs <= 2 * m + 1 < rs + 8
    dr0 = 2 * m - r
    if v0 and v1:
        return 13 - (dr0 + 7)
    if (not v0) and v1:
        assert dr0 + 1 == -4
        return 14
    if v0 and not v1:
        assert dr0 == 3
        return 15
    return 16


def _local_chunks(j):
    ms = set()
    for b in range(2):
        r = 2 * j + b
        rs = min(max(r - 4, 0), 8)
        for rr in range(rs, rs + 8):
            ms.add(rr // 2)
    return sorted(ms)


class _Stop(Exception):
    pass


def build_nc(nlayers=DEPTH, dbg=(), stop=None):
    nc = bass.Bass("TRN2", target_bir_lowering=False)

    def din(name, shape):
        return nc.dram_tensor(name, list(shape), F32, kind="ExternalInput").ap()

    def dout(name, shape):
        return nc.dram_tensor(name, list(shape), F32, kind="ExternalOutput").ap()

    x_d = din("x", [NT, D])
    cvec_d = din("cvec", [128, 8])
    ctxk_d = din("ctxk", [DEPTH, 256, 512])
    ctxv_d = din("ctxv", [DEPTH, 256, 512])
    w_ada_d = din("w_ada", [DEPTH, D, 6 * D])
    w_in_d = din("w_in", [DEPTH, D, D_IN])
    conv_pw_d = din("conv_pw", [DEPTH, 512, D])
    na_out_d = din("na_out", [DEPTH, 512, D])
    gm_out_d = din("gm_out", [DEPTH, 512, D])
    w_o_d = din("w_o", [DEPTH, D, D])
    w1_d = din("moe_w1", [DEPTH, 32, D, 128])
    w3_d = din("moe_w3", [DEPTH, 32, D, 128])
    w2_d = din("moe_w2", [DEPTH, 32, 128, D])
    pv_d = din("pv", [DEPTH, 128, NPV])
    rw_d = din("rw", [DEPTH, 128, 8 * 36])
    rb_d = din("rb", [DEPTH, 128, 36])
    gmw_d = din("gmw", [DEPTH, 128, 512])
    gmbs_d = din("gmbs", [DEPTH, 128, 512])
    gmln_d = din("gmln", [DEPTH, 128, 1024])
    slots_d = din("slots", [DEPTH, 128, 8 * NSLOT * 64])
    rope_d = din("rope", [128, 2 * 8 * 64])
    cf_d = din("cf", [128, 3 * 128 + 3])
    cb_d = din("cb", [128, 128 + 32 * 128 + 128])
    crow_d = din("crow", [1, 256])
    y_d = dout("y", [NT, D])
    nk_d = dout("newk", [DEPTH, NT, 512])
    nv_d = dout("newv", [DEPTH, NT, 512])
    dbg_d = {name: dout("dbg_" + name, shape) for name, shape in dbg}

    from contextlib import ExitStack
    with ExitStack() as es:
        def sb(name, shape, dt=F32):
            return es.enter_context(nc.sbuf_tensor("sb_" + name, list(shape), dt))

        sems = [es.enter_context(nc.semaphore("s%d" % i)) for i in range(40)]
        P = Prog(nc, sems)

        xT = sb("xT", [128, 8, NT])
        hT = sb("hT", [128, 8, NT], BF16)
        NW = 3
        wpool = sb("wpool", [128, NW, 4096], BF16)
        big = sb("big", [128, 8192], F32)
        S1 = sb("S1", [128, 4096], F32)
        S2 = sb("S2", [128, 4 * 4 * 286], F32)
        S3 = sb("S3", [128, 8 * 8 * 65], BF16)
        attT = sb("attT", [128, 4, NT], BF16)
        rope = sb("rope", [128, 2, 8, 64])
        cf = sb("cf", [128, 3 * 128 + 3])
        cbb = sb("cbb", [128, 128 + 32 * 128 + 128], BF16)
        crow = sb("crow", [1, 256], BF16)
        pv = sb("pv", [128, NPV])
        modv = sb("modv", [128, 104])
        scol = sb("scol", [128, 8], BF16)
        cvs = sb("cvs", [128, 8])
        rwb = sb("rwb", [128, 8, 36], BF16)
        rbb = sb("rbb", [128, 36])
        gmw = sb("gmw", [128, 4, 128], BF16)
        gmbs = sb("gmbs", [128, 4, 128])
        gmln = sb("gmln", [128, 2, 512])
        kcT = sb("kcT", [128, 4, 256], BF16)
        vca = sb("vca", [128, 2, 8, 65], BF16)
        PT = sb("PT", [128, 2, 8, 128], BF16)
        atm = sb("atm", [128, 2, 512], BF16)
        rcp = sb("rcp", [128, 2, 8])
        stg = sb("stg", [128, 4, 512])
        stg2 = sb("stg2", [128, 2, 512])
        gtm = sb("gtm", [128, 8, 32])
        gateT = sb("gateT", [32, NT], BF16)
        bnst = sb("bnst", [128, 8])
        modraw = sb("modraw", [128, 48])
        onesb = sb("onesb", [128, 16], BF16)
        zerosb = sb("zerosb", [128, 16], BF16)
        dbuf = sb("dbuf", [128, 4, 128], BF16)

        uo = S2[:].bitcast(BF16)[:, 0:4096].rearrange("p (c n) -> p c n", c=4)
        cact = S2[:].bitcast(BF16)[:, 4096:8192].rearrange("p (c n) -> p c n", c=4)
        mT = S1[:].bitcast(BF16).rearrange("p (c n) -> p c n", c=8)
        slots = big[:].bitcast(BF16)[:, 0:8 * NSLOT * 64].rearrange("p (h s d) -> p h s d", h=8, s=NSLOT)
        kctm = PT[:].rearrange("p a i n -> p (a i n)")[:, 0:1024].rearrange("p (c n) -> p c n", c=2)
        vctm = PT[:].rearrange("p a i n -> p (a i n)")[:, 1024:2048].rearrange("p (c n) -> p c n", c=2)
        lnm = stg
        rt = stg2[:].rearrange("p a n -> p (a n)").rearrange("p (t n) -> p t n", t=8)
        xin = big[:, 0:2048].rearrange("p (b n) -> p b n", b=2)
        ps_all = es.enter_context(nc.psum_tensor("ps", [128, 8, 512], F32))

        def ps(i):
            return ps_all[:, i, :]

        def psb(i):
            return ps_all[:, i, :].bitcast(BF16)

        ident_f = cf[:, 0:128]
        onesD = cf[:, 128:256]
        onesC = cf[:, 256:384]
        flag = cf[:, 384:385]
        epsc = cf[:, 385:386]
        killc = cf[:, 386:387]
        ident_b = cbb[:, 0:128]
        onesDb = cbb[:, 128 + 32 * 128:128 + 32 * 128 + 128]
        ones_row = crow[0:1, 0:128]
        kill_row = crow[0:1, 128:256]

        pscnt = [0]

        reserved = set()

        def nextps():
            pscnt[0] = (pscnt[0] + 1) % 8
            while pscnt[0] in reserved:
                pscnt[0] = (pscnt[0] + 1) % 8
            return pscnt[0]

        wcnt = [0]

        def wload(src_aps, views=None):
            s = wcnt[0] % NW
            wcnt[0] += 1
            for dst_fn, src in src_aps:
                P.dma("pool", "w%d" % s, dst_fn(wpool[:, s, :]), src, w=[("w", s)])
            return s

        def wview(s, pat, **kw):
            return wpool[:, s, :].rearrange(pat, **kw)

        def dump(name, ap, keys):
            if name in dbg_d and not (os.environ.get("KD_NODUMP") and name != "hT"):
                P.dma("pool", "dbg", dbg_d[name], ap, r=keys)

        P.dma("sp", "c0", cf[:], cf_d, w=["cf"])
        P.dma("pool", "c1", cbb[:], cb_d, w=["cbb"])
        P.dma("pool", "c1", crow[:], crow_d, w=["crow"])
        P.dma("sp", "c0", rope[:].rearrange("p a t d -> p (a t d)"), rope_d, w=["rope"])
        P.dma("sp", "c0", cvs[:], cvec_d, w=["cvs"])
        P.op("act", lambda e: e.activation(out=scol[:], in_=cvs[:], func=AF.Silu), r=["cvs"], w=["scol"])
        P.op("dve", lambda e: e.memset(S2[:], 0.0), w=["S2"])
        P.op("dve", lambda e: e.memset(S3[:], 1.0), w=["S3"])
        P.op("dve", lambda e: e.memset(onesb[:], 1.0), w=["onesb"])
        P.op("dve", lambda e: e.memset(zerosb[:], 0.0), w=["zerosb"])
        P.op("dve", lambda e: e.memset(vca[:].rearrange("p a h d -> p (a h d)"), 1.0), w=["vca"])

        for t in range(8):
            b = t % 2
            P.dma("sp", "xin%d" % b, xin[:, b, :], x_d[t * 128:(t + 1) * 128, :], w=[("xin", b)])
            for cg in range(2):
                pi = nextps()
                for c4 in range(4):
                    c = cg * 4 + c4
                    P.op("pe", lambda e, c=c, c4=c4, pi=pi, b=b: e.transpose(
                        out=ps(pi)[:, c4 * 128:(c4 + 1) * 128], in_=xin[:, b, c * 128:(c + 1) * 128],
                        identity=ident_f), r=[("xin", b), "cf"], w=[("ps", pi)])
                P.op("dve" if cg == 0 else "act", lambda e, cg=cg, pi=pi, t=t: e.tensor_scalar_mul(
                    out=xT[:, cg * 4:(cg + 1) * 4, t * 128:(t + 1) * 128],
                    in0=ps(pi).rearrange("p (c n) -> p c n", c=4), scalar1=ALPHA) if cg == 0 else e.activation(
                    out=xT[:, cg * 4:(cg + 1) * 4, t * 128:(t + 1) * 128],
                    in_=ps(pi).rearrange("p (c n) -> p c n", c=4), func=AF.Copy, scale=ALPHA),
                    r=[("ps", pi)], w=[("xT", c) for c in range(cg * 4, cg * 4 + 4)])

        def chk(k):
            if stop is not None and k >= stop:
                raise _Stop()

        XK = [("xT", c) for c in range(8)]
        HK = [("hT", c) for c in range(8)]

        ada = {}

        def adaln_begin():
            ada["pi"] = nextps()
            reserved.add(ada["pi"])

        def adaln_pieces(L, pieces):
            pi = ada["pi"]
            for piece in pieces:
                src = w_ada_d[L].rearrange("(kc p) n -> p kc n", p=128)[:, :, piece * 512:(piece + 1) * 512]
                s = wload([(lambda v: v.rearrange("p (kc n) -> p kc n", kc=8), src)])
                wv = wview(s, "p (kc n) -> p kc n", kc=8)
                for j in range(4):
                    col = piece * 4 + j
                    for kc in range(8):
                        P.op("pe", lambda e, wv=wv, j=j, kc=kc, col=col, pi=pi: e.matmul(
                            out=ps(pi)[:, col:col + 1], lhsT=wv[:, kc, j * 128:(j + 1) * 128],
                            rhs=scol[:, kc:kc + 1], start=(kc == 0), stop=(kc == 7)),
                            r=[("w", s), "scol"], w=[("ps", pi)])

        def adaln_end_raw():
            pi = ada["pi"]
            P.op("dve", lambda e: e.tensor_copy(out=modraw[:], in_=ps(pi)[:, 0:48]),
                 r=[("ps", pi)], w=["modraw"])
            reserved.discard(pi)

        def adaln_finish(L):
            P.dma("sp", "pv", pv[:], pv_d[L], w=["pv"])
            P.op("dve", lambda e: e.tensor_tensor(out=modv[:, 0:48], in0=modraw[:], in1=pv[:, 0:48],
                                                  op=ALU.add), r=["modraw", "pv"], w=["modv"])
            P.op("dve", lambda e: e.tensor_scalar_add(out=modv[:, 8:16], in0=modv[:, 8:16], scalar1=1.0),
                 r=["modv"], w=["modv"])
            P.op("dve", lambda e: e.tensor_scalar_add(out=modv[:, 32:40], in0=modv[:, 32:40], scalar1=1.0),
                 r=["modv"], w=["modv"])
            P.op("dve", lambda e: e.tensor_tensor(out=modv[:, 48:56], in0=pv[:, 48:56], in1=modv[:, 32:40],
                                                  op=ALU.mult), r=["modv", "pv"], w=["modv"])
            P.op("dve", lambda e: e.tensor_tensor(out=modv[:, 56:64], in0=pv[:, 56:64], in1=modv[:, 32:40],
                                                  op=ALU.mult), r=["modv", "pv"], w=["modv"])
            P.op("dve", lambda e: e.tensor_tensor(out=modv[:, 56:64], in0=modv[:, 56:64], in1=modv[:, 24:32],
                                                  op=ALU.add), r=["modv"], w=["modv"])
            P.op("dve", lambda e: e.tensor_scalar_mul(out=modv[:, 64:72], in0=modv[:, 8:16], scalar1=1.0 / ALPHA),
                 r=["modv"], w=["modv"])
            P.op("dve", lambda e: e.tensor_scalar_mul(out=modv[:, 72:88], in0=pv[:, 48:64], scalar1=ALPHA),
                 r=["modv", "pv"], w=["modv"])
            P.op("dve", lambda e: e.tensor_scalar_mul(out=modv[:, 88:104], in0=pv[:, 64:80], scalar1=ALPHA),
                 r=["modv", "pv"], w=["modv"])

        SH1, SC1P, G1, SH2, SC2P, G2, GS, BS2 = 0, 8, 16, 24, 32, 40, 48, 56
        SC1A, AG1, AB1, AG2, AB2 = 64, 72, 80, 88, 96
        LN1G, LN1B, LN2G, LN2B, CVB, CLG, CLB, CDW = 48, 56, 64, 72, 80, 84, 88, 92

        def ln_fm(src_fn, keys, nch, ones_ap, consume):
            sqv = S1[:].bitcast(BF16).rearrange("p (h c n) -> p h c n", h=2, c=8)
            P.alias(S1KEYS, [("sq", c) for c in range(16)])
            banks = []
            for half in range(2):
                pm, pe_ = nextps(), nextps()
                banks.append((pm, pe_))
                for c in range(nch):
                    P.op("act", lambda e, c=c, half=half: e.activation(
                        out=sqv[:, half, c, :], in_=src_fn(c, half), func=AF.Square),
                        r=[keys[c]], w=[("sq", half * 8 + c)])
                for c in range(nch):
                    P.op("pe", lambda e, c=c, half=half, pm=pm: e.matmul(
                        out=ps(pm), lhsT=ones_ap, rhs=src_fn(c, half), start=(c == 0), stop=(c == nch - 1)),
                        r=[keys[c], "cf"], w=[("ps", pm)])
                for c in range(nch):
                    P.op("pe", lambda e, c=c, half=half, pe_=pe_: e.matmul(
                        out=ps(pe_), lhsT=onesDb, rhs=sqv[:, half, c, :], start=(c == 0), stop=(c == nch - 1)),
                        r=[("sq", half * 8 + c), "cbb"], w=[("ps", pe_)])
            for half in range(2):
                pm, pe_ = banks[half]
                mean, rstd, tmp = lnm[:, 0, :], lnm[:, 1, :], lnm[:, 2, :]
                P.op("act", lambda e, pm=pm: e.copy(out=mean, in_=ps(pm)), r=[("ps", pm)], w=[("stg", 0)])
                P.op("dve", lambda e: e.tensor_tensor(out=tmp, in0=mean, in1=mean, op=ALU.mult),
                     r=[("stg", 0)], w=[("stg", 2)])
                P.op("dve", lambda e, pe_=pe_: e.tensor_tensor(out=tmp, in0=ps(pe_), in1=tmp, op=ALU.subtract),
                     r=[("ps", pe_), ("stg", 2)], w=[("stg", 2)])
                P.op("act", lambda e: e.activation(out=rstd, in_=tmp, func=AF.Sqrt, bias=epsc),
                     r=[("stg", 2), "cf"], w=[("stg", 1)])
                P.op("dve", lambda e: e.reciprocal(out=rstd, in_=rstd), r=[("stg", 1)], w=[("stg", 1)])
                for c in range(nch):
                    db = c % 2
                    dap = stg2[:, db, :]
                    P.op("dve", lambda e, c=c, half=half, dap=dap: e.tensor_tensor(
                        out=dap, in0=src_fn(c, half), in1=mean, op=ALU.subtract),
                        r=[keys[c], ("stg", 0)], w=[("stg2", db)])
                    P.op("dve", lambda e, dap=dap: e.tensor_tensor(out=dap, in0=dap, in1=rstd, op=ALU.mult),
                         r=[("stg", 1), ("stg2", db)], w=[("stg2", db)])
                    consume(c, half, dap, ("stg2", db))

        S1KEYS = ([("q_r", t) for t in range(8)] + [("k_r", t) for t in range(8)] + [("u", c) for c in range(4)]
                  + [("sig", c) for c in range(4)] + [("convo", c) for c in range(4)] + [("sq", c) for c in range(16)]
                  + [("mT", c) for c in range(8)])
        S2KEYS = ([("qT", t) for t in range(8)] + [("kT", t) for t in range(8)] + [("ypad", c) for c in range(4)]
                  + [("uo", c) for c in range(4)] + [("cact", c) for c in range(4)])
        PTKEYS = [("PT", a, b) for a in range(2) for b in range(2)] + ["kctm", "vctm"]
        S3KEYS = [("v_aug", t) for t in range(8)] + [("vln", t) for t in range(8)]
        BIGKEYS = ([("bigsq", c) for c in range(4)] + [("merged", c) for c in range(8)]
                   + [("gg", e) for e in range(16)] + ["slots", ("xin", 0), ("xin", 1)])

        def hs(half):
            return slice(half * 512, (half + 1) * 512)

        try:
          chk(1)
          adaln_begin()
          adaln_pieces(0, range(12))
          adaln_end_raw()
          adaln_finish(0)
          chk(2)
          for L in range(nlayers):
              for c in range(8):
                  P.op("act", lambda e, c=c: e.activation(
                      out=hT[:, c, :], in_=xT[:, c, :], func=AF.Identity,
                      scale=modv[:, SC1A + c:SC1A + c + 1], bias=modv[:, SH1 + c:SH1 + c + 1]),
                      r=[("xT", c), "modv"], w=[("hT", c)])
              if L == 0:
                  dump("hT", hT[:, 0, :], HK)

              chk(4)
              q_r = S1[:].bitcast(BF16)[:, 0:4096].rearrange("p (t n) -> p t n", t=8)
              k_r = S1[:].bitcast(BF16)[:, 4096:8192].rearrange("p (t n) -> p t n", t=8)
              qT = S2[:].bitcast(BF16)[:, 0:4096].rearrange("p (c n) -> p c n", c=4)
              kT = S2[:].bitcast(BF16)[:, 4096:8192].rearrange("p (c n) -> p c n", c=4)
              v_aug = S3[:].rearrange("p (t h d) -> p t h d", t=8, h=8)

              def rope_ops(pi, dst, tab, t, rkeys, wkey):
                  src4 = ps(pi).rearrange("p (h b x d) -> p h b x d", h=8, b=2, x=2)
                  A = stg[:, 2, :]
                  B = stg[:, 3, :]
                  A4 = A.rearrange("p (h b x d) -> p h b x d", h=8, b=2, x=2)
                  B4 = B.rearrange("p (h b x d) -> p h b x d", h=8, b=2, x=2)
                  cosv = rope[:, tab, t, :].rearrange("p (b x d) -> p b x d", b=2, x=2)
                  sinv = rope[:, tab + 1, t, :].rearrange("p (b x d) -> p b x d", b=2, x=2)
                  if os.environ.get("KD_NOROPE"):
                      P.op("dve", lambda e: e.tensor_copy(out=dst, in_=ps(pi)), r=rkeys, w=[wkey])
                      return
                  for x in range(2):
                      for bb in range(2):
                          P.op("dve", lambda e, x=x, bb=bb: e.tensor_tensor(
                              out=A4[:, :, bb, x, :], in0=src4[:, :, bb, x, :],
                              in1=cosv[:, bb, x, :].unsqueeze(1).to_broadcast([128, 8, 16]), op=ALU.mult),
                              r=rkeys + ["rope"], w=["ropeA", ("psr", pi)])
                          P.op("dve", lambda e, x=x, bb=bb: e.tensor_tensor(
                              out=B4[:, :, bb, x, :], in0=src4[:, :, bb, 1 - x, :],
                              in1=sinv[:, bb, x, :].unsqueeze(1).to_broadcast([128, 8, 16]), op=ALU.mult),
                              r=rkeys + ["rope"], w=["ropeB", ("psr", pi)])
                  P.op("dve", lambda e: e.tensor_tensor(out=dst, in0=A, in1=B, op=ALU.add),
                       r=["ropeA", "ropeB"], w=[wkey])

              P.alias(S1KEYS, [("q_r", t) for t in range(8)] + [("k_r", t) for t in range(8)])
              P.alias(S2KEYS, [("qT", t) for t in range(8)] + [("kT", t) for t in range(8)])
              P.alias(S3KEYS, [("v_aug", t) for t in range(8)])
              for t in range(0 if not os.environ.get("KD_NOONES") else 0, 8 if not os.environ.get("KD_NOONES") else 0):
                  P.op("dve", lambda e, t=t: e.tensor_copy(out=v_aug[:, t, :, 64:65],
                                                             in_=onesb[:, 0:8].unsqueeze(2)),
                       r=["onesb"], w=[("v_aug", t)])
              for which, c0 in (("k", C_K), ("q", C_Q), ("v", C_V)):
                  if os.environ.get("KD_ONLY") and which not in os.environ.get("KD_ONLY"):
                      continue
                  src = w_in_d[L].rearrange("(kc p) n -> p kc n", p=128)[:, :, c0:c0 + 512]
                  s = wload([(lambda v: v.rearrange("p (kc n) -> p kc n", kc=8), src)])
                  wv = wview(s, "p (kc n) -> p kc n", kc=8)
                  for t in range(8):
                      pi = nextps()
                      for kc in range(8):
                          P.op("pe", lambda e, kc=kc, t=t, pi=pi, wv=wv: e.matmul(
                              out=ps(pi), lhsT=hT[:, kc, t * 128:(t + 1) * 128], rhs=wv[:, kc, :],
                              start=(kc == 0), stop=(kc == 7)), r=[("hT", kc), ("w", s)], w=[("ps", pi)])
                      if which in ("k", "v"):
                          sbi = t % 2
                          P.op("act", lambda e, pi=pi, sbi=sbi: e.copy(out=stg[:, sbi, :], in_=ps(pi)),
                               r=[("ps", pi)], w=[("stg", sbi), ("psr", pi)])
                          dst = (nk_d if which == "k" else nv_d)[L, t * 128:(t + 1) * 128, :]
                          if not os.environ.get("KD_NOSTORE"):
                              P.dma("sp", "o%d" % sbi, dst, stg[:, sbi, :], r=[("stg", sbi)])
                      if which == "k":
                          rope_ops(pi, k_r[:, t, :], 0, t, [("ps", pi)], ("k_r", t))
                      elif which == "q":
                          rope_ops(pi, q_r[:, t, :], 0, t, [("ps", pi)], ("q_r", t))
                      else:
                          P.op("dve", lambda e, pi=pi, t=t: e.tensor_copy(
                              out=v_aug[:, t, :, 0:64], in_=ps(pi).rearrange("p (h d) -> p h d", h=8)),
                              r=[("ps", pi)], w=[("v_aug", t), ("psr", pi)])
              for which in ("k", "q"):
                  if True:
                      srcr, dstT, nm = (k_r, kT, "kT") if which == "k" else (q_r, qT, "qT")
                      for t in range(8):
                          pi = nextps()
                          for c in range(4):
                              P.op("pe", lambda e, c=c, t=t, pi=pi, srcr=srcr: e.transpose(
                                  out=psb(pi)[:, c * 128:(c + 1) * 128], in_=srcr[:, t, c * 128:(c + 1) * 128],
                                  identity=ident_b), r=[(which + "_r", t), "cbb"], w=[("ps", pi)])
                          P.op("act", lambda e, t=t, pi=pi, dstT=dstT: e.activation(
                              out=dstT[:, :, t * 128:(t + 1) * 128],
                              in_=psb(pi)[:, 0:512].rearrange("p (c n) -> p c n", c=4), func=AF.Copy,
                              scale=(0.125 if which == "q" else 1.0)),
                              r=[("ps", pi)], w=[(nm, t)])
              dump("kT", kT[:, 0, :], [("kT", t) for t in range(8)])
              dump("qT", qT[:, 0, :], [("qT", t) for t in range(8)])

              P.alias(BIGKEYS, ["slots"])
              P.alias(PTKEYS, ["kctm", "vctm"])
              P.dma("pool", "sl", slots[:].rearrange("p h s d -> p (h s d)"), slots_d[L], w=["slots"])
              P.dma("pool", "rw", rwb[:].rearrange("p k n -> p (k n)"), rw_d[L], w=["rwb"])
              P.dma("sp", "rb", rbb[:], rb_d[L], w=["rbb"])
              P.dma("pool", "gw", gmw[:].rearrange("p g n -> p (g n)"), gmw_d[L], w=["gmw"])
              P.dma("sp", "gb", gmbs[:].rearrange("p g n -> p (g n)"), gmbs_d[L], w=["gmbs"])
              P.dma("sp", "gl", gmln[:].rearrange("p a n -> p (a n)"), gmln_d[L], w=["gmln"])
              P.dma("pool", "ck", kctm[:], ctxk_d[L].rearrange("(c p) n -> p c n", p=128), w=["kctm"])
              P.dma("pool", "cv", vctm[:], ctxv_d[L].rearrange("(c p) n -> p c n", p=128), w=["vctm"])

              chk(3)
              P.op("dve", lambda e: e.tensor_copy(out=vca[:, :, :, 0:64],
                                                  in_=vctm[:].rearrange("p a (h d) -> p a h d", h=8)),
                   r=["vctm"], w=["vca"])
              for kc2 in range(2):
                  pi = nextps()
                  for c in range(4):
                      P.op("pe", lambda e, c=c, kc2=kc2, pi=pi: e.transpose(
                          out=psb(pi)[:, c * 128:(c + 1) * 128], in_=kctm[:, kc2, c * 128:(c + 1) * 128],
                          identity=ident_b), r=["kctm", "cbb"], w=[("ps", pi)])
                  P.op("act", lambda e, kc2=kc2, pi=pi: e.copy(
                      out=kcT[:, :, kc2 * 128:(kc2 + 1) * 128],
                      in_=psb(pi)[:, 0:512].rearrange("p (c n) -> p c n", c=4)), r=[("ps", pi)], w=["kcT"])

              chk(5)
              P.alias(PTKEYS, [("PT", a, b) for a in range(2) for b in range(2)])
              for j in range(8):
                  lm = _local_chunks(j)
                  vset = (2 * (j // 2), 2 * (j // 2) + 1)
                  tiles = ([("l", m) for m in lm if m in vset] + [("l", m) for m in lm if m not in vset]
                           + [("c", 0), ("c", 1)])
                  assert [t_[1] for t_ in tiles[:2]] == list(vset)
                  pvb = [0, 1]
                  def emit_scores(h):
                      hp, hc = h % 2, h // 2
                      prow = slice(64 * hp, 64 * hp + 64)
                      sbk = [2 + 2 * (h % 2), 3 + 2 * (h % 2)]
                      pb = h % 2
                      for i, (kind, m) in enumerate(tiles):
                          bank = sbk[i // 4]
                          cs = (i % 4) * 128
                          kill = (kind == "c") or (m not in (2 * (j // 2), 2 * (j // 2) + 1))
                          if kind == "l":
                              lhs = kT[prow, hc, m * 128:(m + 1) * 128]
                              rk = [("kT", m)]
                          else:
                              lhs = kcT[prow, hc, m * 128:(m + 1) * 128]
                              rk = ["kcT"]
                          P.op("pe", lambda e, lhs=lhs, bank=bank, cs=cs, kind=kind: e.matmul(
                              out=ps(bank)[:, cs:cs + 128], lhsT=lhs, rhs=qT[prow, hc, j * 128:(j + 1) * 128],
                              start=True, stop=(kind == "c")), r=rk + [("qT", j)], w=[("ps", bank)])
                          if kind == "l":
                              sl0, sl1 = _slot_index(j, m, 0), _slot_index(j, m, 1)
                              if sl1 == sl0 + 1:
                                  P.op("pe", lambda e, bank=bank, cs=cs, sl0=sl0: e.matmul(
                                      out=ps(bank)[:, cs:cs + 128], lhsT=ident_b,
                                      rhs=slots[:, h, sl0:sl0 + 2, :].rearrange("p a d -> p (a d)"),
                                      start=False, stop=True), r=["slots", "cbb"], w=[("ps", bank)])
                              else:
                                  for b in range(2):
                                      sl = (sl0, sl1)[b]
                                      last = (b == 1)
                                      P.op("pe", lambda e, bank=bank, cs=cs, b=b, sl=sl, last=last: e.matmul(
                                          out=ps(bank)[:, cs + b * 64:cs + b * 64 + 64], lhsT=ident_b,
                                          rhs=slots[:, h, sl, :], start=False, stop=last),
                                          r=["slots", "cbb"], w=[("ps", bank)])
                      nt_ = len(tiles)
                      P.op("act", lambda e, pb=pb, sbk=sbk: e.activation(
                          out=PT[:, pb, 0:2, :], in_=ps(sbk[0])[:, 0:256].rearrange("p (i n) -> p i n", i=2),
                          func=AF.Exp), r=[("ps", sbk[0])], w=[("PT", pb, 0)])
                      P.op("act", lambda e, pb=pb, sbk=sbk: e.activation(
                          out=PT[:, pb, 2:4, :], in_=ps(sbk[0])[:, 256:512].rearrange("p (i n) -> p i n", i=2),
                          func=AF.Exp, bias=killc), r=[("ps", sbk[0]), "cf"], w=[("PT", pb, 0)])
                      nb_ = nt_ - 4
                      P.op("act", lambda e, pb=pb, sbk=sbk, nb_=nb_: e.activation(
                          out=PT[:, pb, 4:4 + nb_, :],
                          in_=ps(sbk[1])[:, 0:nb_ * 128].rearrange("p (i n) -> p i n", i=nb_),
                          func=AF.Exp, bias=killc), r=[("ps", sbk[1]), "cf"], w=[("PT", pb, 1)])
                  def emit_pv(h):
                      pb = h % 2
                      nt_ = len(tiles)
                      ob = pvb[h // 4]
                      oc = (h % 4) * 65
                      for i, (kind, m) in enumerate(tiles):
                          if kind == "l":
                              rhs = v_aug[:, m, h, :]
                              rk = [("v_aug", m)]
                          else:
                              rhs = vca[:, m, h, :]
                              rk = ["vca"]
                          P.op("pe", lambda e, i=i, rhs=rhs, ob=ob, oc=oc, pb=pb: e.matmul(
                              out=ps(ob)[:, oc:oc + 65], lhsT=PT[:, pb, i, :], rhs=rhs,
                              start=(i == 0), stop=(i == nt_ - 1)),
                              r=rk + [("PT", pb, i // 4)], w=[("ps", ob)])
                  emit_scores(0)
                  for h in range(8):
                      if h + 1 < 8:
                          emit_scores(h + 1)
                      emit_pv(h)
                  ab = j % 2
                  for hb in range(2):
                      o3 = ps(pvb[hb])[:, 0:260].rearrange("p (h d) -> p h d", h=4)
                      P.op("dve", lambda e, o3=o3, hb=hb, ab=ab: e.reciprocal(
                          out=rcp[:, ab, hb * 4:hb * 4 + 4], in_=o3[:, :, 64]),
                          r=[("ps", pvb[hb])], w=[("rcp", ab, hb)])
                      P.op("dve", lambda e, o3=o3, hb=hb, ab=ab: e.tensor_tensor(
                          out=atm[:, ab, hb * 256:(hb + 1) * 256].rearrange("p (h d) -> p h d", h=4),
                          in0=o3[:, :, 0:64],
                          in1=rcp[:, ab, hb * 4:hb * 4 + 4].unsqueeze(2).to_broadcast([128, 4, 64]),
                          op=ALU.mult), r=[("ps", pvb[hb]), ("rcp", ab, hb)], w=[("atm", ab)])
                  pi = 6 + (j % 2)
                  for c in range(4):
                      P.op("pe", lambda e, c=c, pi=pi, ab=ab: e.transpose(
                          out=psb(pi)[:, c * 128:(c + 1) * 128], in_=atm[:, ab, c * 128:(c + 1) * 128],
                          identity=ident_b), r=[("atm", ab), "cbb"], w=[("ps", pi)])
                  P.op("act", lambda e, j=j, pi=pi: e.copy(
                      out=attT[:, :, j * 128:(j + 1) * 128],
                      in_=psb(pi)[:, 0:512].rearrange("p (c n) -> p c n", c=4)),
                      r=[("ps", pi)], w=[("attT", j)])
              ATK = [("attT", j) for j in range(8)]
              dump("attT", attT[:, 0, :], ATK)

              chk(6)
              sig = S1[:].rearrange("p (c n) -> p c n", c=4)
              ypad = S2[:].bitcast(BF16)[:, 0:4 * 4 * 286].rearrange("p (c s n) -> p c s n", c=4, s=4)
              P.alias(S1KEYS, [("sig", c) for c in range(4)])
              src = w_in_d[L].rearrange("(kc p) n -> p kc n", p=128)[:, :, C_CB:C_CB + 512]
              s = wload([(lambda v: v.rearrange("p (kc n) -> p kc n", kc=8), src)])
              wv = wview(s, "p (kc n) -> p kc n", kc=8)
              for jj in range(4):
                  for half in range(2):
                      pi = nextps()
                      for kc in range(8):
                          P.op("pe", lambda e, kc=kc, jj=jj, half=half, pi=pi, wv=wv: e.matmul(
                              out=ps(pi), lhsT=wv[:, kc, jj * 128:(jj + 1) * 128], rhs=hT[:, kc, hs(half)],
                              start=(kc == 0), stop=(kc == 7)), r=[("hT", kc), ("w", s)], w=[("ps", pi)])
                      P.op("act", lambda e, jj=jj, half=half, pi=pi: e.activation(
                          out=sig[:, jj, hs(half)], in_=ps(pi), func=AF.Sigmoid),
                          r=[("ps", pi)], w=[("sig", jj)])
              src = w_in_d[L].rearrange("(kc p) n -> p kc n", p=128)[:, :, C_CA:C_CA + 512]
              s = wload([(lambda v: v.rearrange("p (kc n) -> p kc n", kc=8), src)])
              wv = wview(s, "p (kc n) -> p kc n", kc=8)
              P.alias(S2KEYS, [("ypad", c) for c in range(4)])
              for jj in range(4):
                  P.op("dve", lambda e, jj=jj: e.tensor_copy(out=ypad[:, jj, 0, 0:15], in_=zerosb[:, 0:15]),
                       r=["zerosb"], w=[("ypad", jj)])
                  P.op("dve", lambda e, jj=jj: e.tensor_copy(out=ypad[:, jj, 3, 271:286], in_=zerosb[:, 0:15]),
                       r=["zerosb"], w=[("ypad", jj)])
              for jj in range(4):
                  for half in range(2):
                      pi = nextps()
                      for kc in range(8):
                          P.op("pe", lambda e, kc=kc, jj=jj, half=half, pi=pi, wv=wv: e.matmul(
                              out=ps(pi), lhsT=wv[:, kc, jj * 128:(jj + 1) * 128], rhs=hT[:, kc, hs(half)],
                              start=(kc == 0), stop=(kc == 7)), r=[("hT", kc), ("w", s)], w=[("ps", pi)])
                      P.op("dve", lambda e, jj=jj, half=half, pi=pi: e.tensor_tensor(
                          out=ypad[:, jj, 2 * half:2 * half + 2, 15:271],
                          in0=ps(pi).rearrange("p (s n) -> p s n", s=2),
                          in1=sig[:, jj, hs(half)].rearrange("p (s n) -> p s n", s=2), op=ALU.mult),
                          r=[("ps", pi), ("sig", jj)], w=[("ypad", jj)])
              convo = S1[:].rearrange("p (c n) -> p c n", c=4)
              P.alias(S1KEYS, [("convo", c) for c in range(4)])
              for jj in range(4):
                  P.op("dve", lambda e, jj=jj: e.tensor_scalar_mul(
                      out=ypad[:, jj, 1:4, 0:15], in0=ypad[:, jj, 0:3, 256:271], scalar1=flag),
                      r=[("ypad", jj), "cf"], w=[("ypad", jj)])
                  P.op("dve", lambda e, jj=jj: e.tensor_scalar_mul(
                      out=ypad[:, jj, 0:3, 271:286], in0=ypad[:, jj, 1:4, 15:30], scalar1=flag),
                      r=[("ypad", jj), "cf"], w=[("ypad", jj)])
              dcnt = 0
              for jj in range(4):
                  pbk = [nextps(), nextps()]
                  for k in range(31):
                      di = dcnt % 4
                      dcnt += 1
                      wk = pv[:, CDW + jj * 31 + k:CDW + jj * 31 + k + 1]
                      P.op("act", lambda e, di=di, wk=wk: e.activation(
                          out=dbuf[:, di, :], in_=ident_b, func=AF.Copy, scale=wk),
                          r=["cbb", "pv"], w=[("dbuf", di)])
                      for half in range(2):
                          P.op("pe", lambda e, jj=jj, k=k, half=half, di=di, pbk=pbk: e.matmul(
                              out=ps(pbk[half]).rearrange("p (s n) -> p s n", s=2), lhsT=dbuf[:, di, :],
                              rhs=ypad[:, jj, 2 * half:2 * half + 2, k:k + 256],
                              start=(k == 0), stop=(k == 30)),
                              r=[("dbuf", di), ("ypad", jj)], w=[("ps", pbk[half])])
                  for half in range(2):
                      P.op("act", lambda e, jj=jj, half=half, pbk=pbk: e.activation(
                          out=convo[:, jj, hs(half)], in_=ps(pbk[half]), func=AF.Identity,
                          bias=pv[:, CVB + jj:CVB + jj + 1]), r=[("ps", pbk[half]), "pv"], w=[("convo", jj)])
              CVK = [("convo", jj) for jj in range(4)]
              dump("convo", convo[:, 0, :], CVK)

              def conv_ln():
                  sqb = big[:].rearrange("p (c n) -> p c n", c=16)
                  P.alias(BIGKEYS, [("bigsq", c) for c in range(4)])
                  for half in range(2):
                      pm, pe_ = nextps(), nextps()
                      for c in range(4):
                          P.op("act", lambda e, c=c, half=half: e.activation(
                              out=sqb[:, c, :], in_=convo[:, c, hs(half)], func=AF.Square),
                              r=[("convo", c)], w=[("bigsq", c)])
                      for c in range(4):
                          P.op("pe", lambda e, c=c, half=half, pm=pm: e.matmul(
                              out=ps(pm), lhsT=onesC, rhs=convo[:, c, hs(half)], start=(c == 0), stop=(c == 3)),
                              r=[("convo", c), "cf"], w=[("ps", pm)])
                      for c in range(4):
                          P.op("pe", lambda e, c=c, pe_=pe_: e.matmul(
                              out=ps(pe_), lhsT=onesC, rhs=sqb[:, c, :], start=(c == 0), stop=(c == 3)),
                              r=[("bigsq", c), "cf"], w=[("ps", pe_)])
                      mean, rstd, tmp = lnm[:, 0, :], lnm[:, 1, :], lnm[:, 2, :]
                      P.op("act", lambda e, pm=pm: e.copy(out=mean, in_=ps(pm)), r=[("ps", pm)], w=[("stg", 0)])
                      P.op("dve", lambda e: e.tensor_tensor(out=tmp, in0=mean, in1=mean, op=ALU.mult),
                           r=[("stg", 0)], w=[("stg", 2)])
                      P.op("dve", lambda e, pe_=pe_: e.tensor_tensor(out=tmp, in0=ps(pe_), in1=tmp,
                                                                    op=ALU.subtract),
                           r=[("ps", pe_), ("stg", 2)], w=[("stg", 2)])
                      P.op("act", lambda e: e.activation(out=rstd, in_=tmp, func=AF.Sqrt, bias=epsc),
                           r=[("stg", 2), "cf"], w=[("stg", 1)])
                      P.op("dve", lambda e: e.reciprocal(out=rstd, in_=rstd), r=[("stg", 1)], w=[("stg", 1)])
                      for c in range(4):
                          db = c % 2
                          dap = stg2[:, db, :]
                          P.op("dve", lambda e, c=c, half=half, dap=dap: e.tensor_tensor(
                              out=dap, in0=convo[:, c, hs(half)], in1=mean, op=ALU.subtract),
                              r=[("convo", c), ("stg", 0)], w=[("stg2", db)])
                          P.op("dve", lambda e, dap=dap: e.tensor_tensor(out=dap, in0=dap, in1=rstd, op=ALU.mult),
                               r=[("stg", 1), ("stg2", db)], w=[("stg2", db)])
                          P.op("act", lambda e, c=c, half=half, dap=dap: e.activation(
                              out=cact[:, c, hs(half)], in_=dap, func=AF.Silu, scale=pv[:, CLG + c:CLG + c + 1],
                              bias=pv[:, CLB + c:CLB + c + 1]), r=[("stg2", db), "pv"], w=[("cact", c)])
              P.alias(S2KEYS, [("cact", c) for c in range(4)])
              conv_ln()
              CAK = [("cact", c) for c in range(4)]
              dump("cact", cact[:, 0, :], CAK)

              chk(7)
              vln = S3[:, 0:4096].rearrange("p (t n) -> p t n", t=8)
              u = S1[:].rearrange("p (c n) -> p c n", c=4)
              P.alias(S3KEYS, [("vln", t) for t in range(8)])

              def gelu_from_ps(pi, dst, dkeys, tmpi):
                  P.op("act", lambda e: e.activation(out=dst, in_=ps(pi), func=AF.Gelu_apprx_tanh),
                       r=[("ps", pi)], w=dkeys)

              src = w_in_d[L].rearrange("(kc p) n -> p kc n", p=128)[:, :, C_GV:C_GV + 512]
              s = wload([(lambda v: v.rearrange("p (kc n) -> p kc n", kc=8), src)])
              wv = wview(s, "p (kc n) -> p kc n", kc=8)
              for t in range(8):
                  pi = nextps()
                  for kc in range(8):
                      P.op("pe", lambda e, kc=kc, t=t, pi=pi, wv=wv: e.matmul(
                          out=ps(pi), lhsT=hT[:, kc, t * 128:(t + 1) * 128], rhs=wv[:, kc, :],
                          start=(kc == 0), stop=(kc == 7)), r=[("hT", kc), ("w", s)], w=[("ps", pi)])
                  gi = t % 2
                  gv_ = stg2[:, gi, :]
                  gelu_from_ps(pi, gv_, [("stg2", gi)], gi)
                  P.op("dve", lambda e, gv_=gv_: e.bn_stats(out=bnst[:, 0:6], in_=gv_),
                       r=[("stg2", gi)], w=["bnst"])
                  P.op("dve", lambda e: e.bn_aggr(out=bnst[:, 6:8], in_=bnst[:, 0:6]), r=["bnst"], w=["bnag"])
                  P.op("act", lambda e: e.activation(out=bnst[:, 7:8], in_=bnst[:, 7:8], func=AF.Sqrt, bias=epsc),
                       r=["bnag", "cf"], w=["bnag"])
                  P.op("dve", lambda e: e.reciprocal(out=bnst[:, 7:8], in_=bnst[:, 7:8]), r=["bnag"], w=["bnag"])
                  P.op("dve", lambda e, gv_=gv_: e.tensor_scalar(
                      out=gv_, in0=gv_, scalar1=bnst[:, 6:7], scalar2=bnst[:, 7:8],
                      op0=ALU.subtract, op1=ALU.mult), r=["bnag", ("stg2", gi)], w=[("stg2", gi)])
                  P.op("dve", lambda e, gv_=gv_: e.tensor_tensor(out=gv_, in0=gv_, in1=gmln[:, 0, :], op=ALU.mult),
                       r=["gmln", ("stg2", gi)], w=[("stg2", gi)])
                  P.op("dve", lambda e, gv_=gv_, t=t: e.tensor_tensor(
                      out=vln[:, t, :], in0=gv_, in1=gmln[:, 1, :], op=ALU.add),
                      r=["gmln", ("stg2", gi)], w=[("vln", t)])

              src = w_in_d[L].rearrange("(kc p) n -> p kc n", p=128)[:, :, C_GU:C_GU + 512]
              s = wload([(lambda v: v.rearrange("p (kc n) -> p kc n", kc=8), src)])
              wv = wview(s, "p (kc n) -> p kc n", kc=8)
              P.alias(S1KEYS, [("u", c) for c in range(4)])
              for jj in range(4):
                  for half in range(2):
                      pi = nextps()
                      for kc in range(8):
                          P.op("pe", lambda e, kc=kc, jj=jj, half=half, pi=pi, wv=wv: e.matmul(
                              out=ps(pi), lhsT=wv[:, kc, jj * 128:(jj + 1) * 128], rhs=hT[:, kc, hs(half)],
                              start=(kc == 0), stop=(kc == 7)), r=[("hT", kc), ("w", s)], w=[("ps", pi)])
                      gelu_from_ps(pi, u[:, jj, hs(half)], [("u", jj)], half)
              P.alias(S2KEYS, [("uo", c) for c in range(4)])
              for g in range(4):
                  for nb in range(2):
                      pi = nextps()
                      for n4 in range(4):
                          n = nb * 4 + n4
                          P.op("pe", lambda e, g=g, n=n, n4=n4, pi=pi: e.matmul(
                              out=ps(pi)[:, n4 * 128:(n4 + 1) * 128], lhsT=vln[:, n, g * 128:(g + 1) * 128],
                              rhs=gmw[:, g, :], start=True, stop=True), r=[("vln", n), "gmw"], w=[("ps", pi)])
                      tb = nb
                      P.op("dve", lambda e, g=g, pi=pi, tb=tb: e.tensor_tensor(
                          out=stg[:, tb, :].rearrange("p (a n) -> p a n", a=4),
                          in0=ps(pi).rearrange("p (a n) -> p a n", a=4),
                          in1=gmbs[:, g, :].unsqueeze(1).to_broadcast([128, 4, 128]), op=ALU.add),
                          r=[("ps", pi), "gmbs"], w=[("stg", tb)])
                      P.op("dve", lambda e, g=g, nb=nb, tb=tb: e.tensor_tensor(
                          out=uo[:, g, nb * 512:(nb + 1) * 512], in0=stg[:, tb, :],
                          in1=u[:, g, nb * 512:(nb + 1) * 512], op=ALU.mult),
                          r=[("stg", tb), ("u", g)], w=[("uo", g)])
              UOK = [("uo", g) for g in range(4)]
              dump("uo", uo[:, 0, :], UOK)

              chk(8)
              merged = big[:].rearrange("p (c n) -> p c n", c=8)
              P.alias(S1KEYS, [("mT", c) for c in range(8)])
              P.alias(BIGKEYS, [("merged", c) for c in range(8)])
              branches = [(1, na_out_d, attT, ATK), (2, gm_out_d, uo, UOK), (0, conv_pw_d, cact, CAK)]
              for bi_, (bidx, wd, actT, akeys) in enumerate(branches):
                  sp_ = wload([(lambda v: v.rearrange("p (cc n) -> p cc n", cc=4),
                                wd[L].rearrange("(cc p) n -> p cc n", p=128))])
                  pw = wview(sp_, "p (cc n) -> p cc n", cc=4)
                  for piece in range(2):
                      c0 = C_GZ + bidx * 1024 + piece * 512
                      src = w_in_d[L].rearrange("(kc p) n -> p kc n", p=128)[:, :, c0:c0 + 512]
                      s = wload([(lambda v: v.rearrange("p (kc n) -> p kc n", kc=8), src)])
                      wv = wview(s, "p (kc n) -> p kc n", kc=8)
                      for jj in range(4):
                          c = piece * 4 + jj
                          for half in range(2):
                              pa, pb_ = nextps(), nextps()
                              for kc in range(8):
                                  P.op("pe", lambda e, kc=kc, jj=jj, half=half, pa=pa, wv=wv: e.matmul(
                                      out=ps(pa), lhsT=wv[:, kc, jj * 128:(jj + 1) * 128], rhs=hT[:, kc, hs(half)],
                                      start=(kc == 0), stop=(kc == 7)), r=[("hT", kc), ("w", s)], w=[("ps", pa)])
                              for cc in range(4):
                                  P.op("pe", lambda e, cc=cc, c=c, half=half, pb_=pb_, pw=pw, actT=actT: e.matmul(
                                      out=ps(pb_), lhsT=pw[:, cc, c * 128:(c + 1) * 128], rhs=actT[:, cc, hs(half)],
                                      start=(cc == 0), stop=(cc == 3)), r=akeys + [("w", sp_)], w=[("ps", pb_)])
                              si = half
                              P.op("act", lambda e, pa=pa, si=si: e.activation(
                                  out=stg[:, si, :], in_=ps(pa), func=AF.Sigmoid),
                                  r=[("ps", pa)], w=[("stg", si)])
                              if bi_ == 0:
                                  P.op("dve", lambda e, c=c, half=half, pb_=pb_, si=si: e.tensor_tensor(
                                      out=merged[:, c, hs(half)], in0=ps(pb_), in1=stg[:, si, :], op=ALU.mult),
                                      r=[("ps", pb_), ("stg", si)], w=[("merged", c)])
                              else:
                                  P.op("dve", lambda e, pb_=pb_, si=si: e.tensor_tensor(
                                      out=stg[:, si, :], in0=ps(pb_), in1=stg[:, si, :], op=ALU.mult),
                                      r=[("ps", pb_), ("stg", si)], w=[("stg", si)])
                                  if bi_ == 1:
                                      P.op("dve", lambda e, c=c, half=half, si=si: e.tensor_tensor(
                                          out=merged[:, c, hs(half)], in0=merged[:, c, hs(half)],
                                          in1=stg[:, si, :], op=ALU.add),
                                          r=[("merged", c), ("stg", si)], w=[("merged", c)])
                                  else:
                                      P.op("dve", lambda e, c=c, half=half, si=si: e.tensor_tensor(
                                          out=mT[:, c, hs(half)], in0=merged[:, c, hs(half)],
                                          in1=stg[:, si, :], op=ALU.add),
                                          r=[("merged", c), ("stg", si)], w=[("mT", c)])
              MTK = [("mT", c) for c in range(8)]
              dump("mT", mT[:, 0, :], MTK)

              chk(9)
              for piece in range(2):
                  src = w_o_d[L].rearrange("(kc p) n -> p kc n", p=128)[:, :, piece * 512:(piece + 1) * 512]
                  s = wload([(lambda v: v.rearrange("p (kc n) -> p kc n", kc=8), src)])
                  wv = wview(s, "p (kc n) -> p kc n", kc=8)
                  for jj in range(4):
                      c = piece * 4 + jj
                      for half in range(2):
                          pi = nextps()
                          for kc in range(8):
                              P.op("pe", lambda e, kc=kc, jj=jj, half=half, pi=pi, wv=wv: e.matmul(
                                  out=ps(pi), lhsT=wv[:, kc, jj * 128:(jj + 1) * 128], rhs=mT[:, kc, hs(half)],
                                  start=(kc == 0), stop=(kc == 7)), r=[("mT", kc), ("w", s)], w=[("ps", pi)])
                          P.op("dve", lambda e, c=c, half=half, pi=pi: e.scalar_tensor_tensor(
                              out=xT[:, c, hs(half)], in0=ps(pi), scalar=modv[:, G1 + c:G1 + c + 1],
                              in1=xT[:, c, hs(half)], op0=ALU.mult, op1=ALU.add),
                              r=[("ps", pi), "modv", ("xT", c)], w=[("xT", c)])
              dump("r1", xT[:, 0, :], XK)

              chk(10)
              def cons1(c, half, dap, dkey):
                  P.op("act", lambda e: e.activation(
                      out=hT[:, c, hs(half)], in_=dap, func=AF.Identity, scale=modv[:, GS + c:GS + c + 1],
                      bias=modv[:, BS2 + c:BS2 + c + 1]), r=[dkey, "modv"], w=[("hT", c)])
                  P.op("act", lambda e: e.activation(
                      out=xT[:, c, hs(half)], in_=dap, func=AF.Identity, scale=modv[:, AG1 + c:AG1 + c + 1],
                      bias=modv[:, AB1 + c:AB1 + c + 1]), r=[dkey, "modv"], w=[("xT", c)])
              ln_fm(lambda c, half: xT[:, c, hs(half)], XK, 8, onesD, cons1)
              dump("tT", hT[:, 0, :], HK)

              chk(11)
              NXT = L + 1 < nlayers
              if NXT:
                  adaln_begin()
                  adaln_pieces(L + 1, range(0, 2))
              lg = rt[:, :, 0:36]
              for t in range(8):
                  pi = nextps()
                  for kc in range(8):
                      P.op("pe", lambda e, kc=kc, t=t, pi=pi: e.matmul(
                          out=ps(pi)[:, 0:36], lhsT=hT[:, kc, t * 128:(t + 1) * 128], rhs=rwb[:, kc, :],
                          start=(kc == 0), stop=(kc == 7)), r=[("hT", kc), "rwb"], w=[("ps", pi)])
                  P.op("dve", lambda e, t=t, pi=pi: e.tensor_tensor(
                      out=rt[:, t, 0:36], in0=ps(pi)[:, 0:36], in1=rbb[:], op=ALU.add),
                      r=[("ps", pi), "rbb"], w=[("stg2", 0), ("stg2", 1)])
              gl = rt[:, :, 0:4]
              el = rt[:, :, 4:36]
              gmax, oneh, gsum, esel = rt[:, :, 36:37], rt[:, :, 40:44], rt[:, :, 37:38], rt[:, :, 44:52]
              prod = rt[:, :, 52:84]
              m1, eq1, e2, m2 = rt[:, :, 38:39], rt[:, :, 84:92], rt[:, :, 92:100], rt[:, :, 39:40]
              msk, ex, den, g8 = rt[:, :, 116:124], rt[:, :, 108:116], rt[:, :, 100:101], rt[:, :, 120:128]
              gex = rt[:, :, 102:106]

              def R(fn):
                  P.op("dve", fn, r=[("stg2", 0), ("stg2", 1)], w=[("stg2", 0), ("stg2", 1)])
              R(lambda e: e.tensor_reduce(out=gmax, in_=gl, axis=AX.X, op=ALU.max))
              R(lambda e: e.tensor_tensor(out=oneh, in0=gl, in1=gmax.to_broadcast([128, 8, 4]), op=ALU.is_equal))
              R(lambda e: e.tensor_tensor(out=gex, in0=gl, in1=gmax.to_broadcast([128, 8, 4]), op=ALU.subtract))
              P.op("act", lambda e: e.activation(out=gex, in_=gex, func=AF.Exp), r=[("stg2", 0), ("stg2", 1)], w=[("stg2", 0), ("stg2", 1)])
              R(lambda e: e.tensor_reduce(out=gsum, in_=gex, axis=AX.X, op=ALU.add))
              R(lambda e: e.tensor_tensor(
                  out=prod.rearrange("p t (g x) -> p t g x", g=4), in0=el.rearrange("p t (g x) -> p t g x", g=4),
                  in1=oneh.unsqueeze(3).to_broadcast([128, 8, 4, 8]), op=ALU.mult))
              R(lambda e: e.tensor_reduce(out=esel, in_=prod.rearrange("p t (g x) -> p t x g", g=4),
                                          axis=AX.X, op=ALU.add))
              R(lambda e: e.tensor_reduce(out=m1, in_=esel, axis=AX.X, op=ALU.max))
              R(lambda e: e.tensor_tensor(out=eq1, in0=esel, in1=m1.to_broadcast([128, 8, 8]), op=ALU.is_equal))
              R(lambda e: e.scalar_tensor_tensor(out=e2, in0=eq1, scalar=NEG, in1=esel, op0=ALU.mult, op1=ALU.add))
              R(lambda e: e.tensor_reduce(out=m2, in_=e2, axis=AX.X, op=ALU.max))
              R(lambda e: e.tensor_tensor(out=msk, in0=esel, in1=m2.to_broadcast([128, 8, 8]), op=ALU.is_ge))
              R(lambda e: e.tensor_tensor(out=ex, in0=esel, in1=m1.to_broadcast([128, 8, 8]), op=ALU.subtract))
              P.op("act", lambda e: e.activation(out=ex, in_=ex, func=AF.Exp), r=[("stg2", 0), ("stg2", 1)], w=[("stg2", 0), ("stg2", 1)])
              R(lambda e: e.tensor_tensor(out=ex, in0=ex, in1=msk, op=ALU.mult))
              R(lambda e: e.tensor_reduce(out=den, in_=ex, axis=AX.X, op=ALU.add))
              R(lambda e: e.tensor_tensor(out=den, in0=den, in1=gsum, op=ALU.mult))
              R(lambda e: e.reciprocal(out=den, in_=den))
              R(lambda e: e.tensor_tensor(out=g8, in0=ex, in1=den.to_broadcast([128, 8, 8]), op=ALU.mult))
              P.op("dve", lambda e: e.tensor_tensor(
                  out=gtm[:].rearrange("p t (g x) -> p t g x", g=4),
                  in0=oneh.unsqueeze(3).to_broadcast([128, 8, 4, 8]),
                  in1=g8.unsqueeze(2).to_broadcast([128, 8, 4, 8]), op=ALU.mult), r=[("stg2", 0), ("stg2", 1)], w=["gtm"])
              dump("gtm", gtm[:].rearrange("p t e -> p (t e)"), ["gtm"])
              if NXT:
                  adaln_pieces(L + 1, range(2, 4))
              for tg in range(2):
                  pi = nextps()
                  for t4 in range(4):
                      t = tg * 4 + t4
                      P.op("pe", lambda e, t=t, t4=t4, pi=pi: e.transpose(
                          out=ps(pi)[0:32, t4 * 128:(t4 + 1) * 128], in_=gtm[:, t, :], identity=ident_f),
                          r=["gtm", "cf"], w=[("ps", pi)])
                  P.op("act", lambda e, tg=tg, pi=pi: e.copy(out=gateT[:, tg * 512:(tg + 1) * 512],
                                                             in_=ps(pi)[0:32, :]),
                       r=[("ps", pi)], w=["gateT"])

              chk(12)
              g_grp = big[:].bitcast(BF16).rearrange("p (e n) -> p e n", e=16)
              P.alias(BIGKEYS, [("gg", e) for e in range(16)])
              for grp in range(2):
                  for ep in range(8):
                      e0 = grp * 16 + ep * 2
                      s = wload([
                          (lambda v: v.rearrange("p (a e kc f) -> p a e kc f", a=2, e=2, kc=8)[:, 0],
                           w1_d[L, e0:e0 + 2].rearrange("e (kc p) f -> p e kc f", p=128)),
                          (lambda v: v.rearrange("p (a e kc f) -> p a e kc f", a=2, e=2, kc=8)[:, 1],
                           w3_d[L, e0:e0 + 2].rearrange("e (kc p) f -> p e kc f", p=128)),
                      ])
                      wv = wview(s, "p (a e kc f) -> p a e kc f", a=2, e=2, kc=8)
                      for ei in range(2):
                          e_glob = e0 + ei
                          e_loc = ep * 2 + ei
                          for half in range(2):
                              p1, p3, pg = nextps(), nextps(), nextps()
                              for kc in range(8):
                                  P.op("pe", lambda e, kc=kc, ei=ei, half=half, p1=p1, wv=wv: e.matmul(
                                      out=ps(p1), lhsT=wv[:, 0, ei, kc, :], rhs=hT[:, kc, hs(half)],
                                      start=(kc == 0), stop=(kc == 7)), r=[("hT", kc), ("w", s)], w=[("ps", p1)])
                              for kc in range(8):
                                  P.op("pe", lambda e, kc=kc, ei=ei, half=half, p3=p3, wv=wv: e.matmul(
                                      out=ps(p3), lhsT=wv[:, 1, ei, kc, :], rhs=hT[:, kc, hs(half)],
                                      start=(kc == 0), stop=(kc == 7)), r=[("hT", kc), ("w", s)], w=[("ps", p3)])
                              P.op("pe", lambda e, e_glob=e_glob, half=half, pg=pg: e.matmul(
                                  out=ps(pg), lhsT=cbb[0:32, 128 + e_glob * 128:128 + (e_glob + 1) * 128],
                                  rhs=gateT[:, hs(half)], start=True, stop=True),
                                  r=["gateT", "cbb"], w=[("ps", pg)])
                              si = half
                              P.op("act", lambda e, p1=p1, si=si: e.activation(
                                  out=stg[:, si, :], in_=ps(p1), func=AF.Silu), r=[("ps", p1)], w=[("stg", si)])
                              P.op("dve", lambda e, p3=p3, si=si: e.tensor_tensor(
                                  out=stg[:, si, :], in0=ps(p3), in1=stg[:, si, :], op=ALU.mult),
                                  r=[("ps", p3), ("stg", si)], w=[("stg", si)])
                              P.op("dve", lambda e, pg=pg, si=si, e_loc=e_loc, half=half: e.tensor_tensor(
                                  out=g_grp[:, e_loc, hs(half)], in0=ps(pg), in1=stg[:, si, :], op=ALU.mult),
                                  r=[("ps", pg), ("stg", si)], w=[("gg", e_loc)])
                      if grp == 0 and NXT:
                          adaln_pieces(L + 1, [4 + ep])
                  if grp == 0 and NXT:
                      adaln_end_raw()
                  for pss in range(2):
                      ss = []
                      for e8 in range(2):
                          e0 = grp * 16 + e8 * 8
                          ss.append(wload([(lambda v: v.rearrange("p (e n) -> p e n", e=8),
                                            w2_d[L, e0:e0 + 8, :, pss * 512:(pss + 1) * 512].rearrange(
                                                "e f n -> f e n"))]))
                      for e_loc in range(16):
                          wv = wview(ss[e_loc // 8], "p (e n) -> p e n", e=8)
                          for dc in range(4):
                              for half in range(2):
                                  bk = dc * 2 + half
                                  P.op("pe", lambda e, e_loc=e_loc, dc=dc, half=half, bk=bk, wv=wv: e.matmul(
                                      out=ps(bk), lhsT=wv[:, e_loc % 8, dc * 128:(dc + 1) * 128],
                                      rhs=g_grp[:, e_loc, hs(half)], start=(e_loc == 0), stop=(e_loc == 15)),
                                      r=[("gg", e_loc), ("w", ss[e_loc // 8])], w=[("ps", bk)])
                      for dc in range(4):
                          c = pss * 4 + dc
                          for half in range(2):
                              bk = dc * 2 + half
                              P.op("dve", lambda e, c=c, half=half, bk=bk: e.scalar_tensor_tensor(
                                  out=xT[:, c, hs(half)], in0=ps(bk), scalar=modv[:, G2 + c:G2 + c + 1],
                                  in1=xT[:, c, hs(half)], op0=ALU.mult, op1=ALU.add),
                                  r=[("ps", bk), "modv", ("xT", c)], w=[("xT", c)])
              dump("r2", xT[:, 0, :], XK)

              chk(13)
              lg_, lb_ = LN2G, LN2B

              def cons2(c, half, dap, dkey):
                  P.op("act", lambda e: e.activation(
                      out=xT[:, c, hs(half)], in_=dap, func=AF.Identity,
                      scale=(modv[:, AG2 + c:AG2 + c + 1] if L + 1 < nlayers else pv[:, lg_ + c:lg_ + c + 1]),
                      bias=(modv[:, AB2 + c:AB2 + c + 1] if L + 1 < nlayers else pv[:, lb_ + c:lb_ + c + 1])),
                      r=[dkey, "pv", "modv"], w=[("xT", c)])
              ln_fm(lambda c, half: xT[:, c, hs(half)], XK, 8, onesD, cons2)
              if L + 1 < nlayers:
                  adaln_finish(L + 1)

        except _Stop:
            pass
        P.alias(BIGKEYS, [("xin", 0), ("xin", 1)])
        for t in range(8):
            b = t % 2
            for cg in range(2):
                pi = nextps()
                for c4 in range(4):
                    c = cg * 4 + c4
                    P.op("pe", lambda e, c=c, c4=c4, pi=pi, t=t: e.transpose(
                        out=ps(pi)[:, c4 * 128:(c4 + 1) * 128], in_=xT[:, c, t * 128:(t + 1) * 128],
                        identity=ident_f), r=[("xT", c), "cf"], w=[("ps", pi)])
                P.op("act", lambda e, cg=cg, pi=pi, b=b: e.copy(out=xin[:, b, cg * 512:(cg + 1) * 512], in_=ps(pi)),
                     r=[("ps", pi)], w=[("xin", b)])
            P.dma("sp", "xo%d" % b, y_d[t * 128:(t + 1) * 128, :], xin[:, b, :], r=[("xin", b)])
        P.finish()
    return nc


def _rope_tables(sample):
    t = np.arange(NT)
    out = np.zeros((128, 2, 8, 64), np.float32)
    if sample:
        pr = (t // 64).astype(np.float32)
        pc = (t % 64).astype(np.float32)
        inv = (10000.0 ** (-np.arange(0, 32, 2, dtype=np.float32) / 32)).astype(np.float32)
        cosv = np.zeros((NT, 64), np.float32)
        sinv = np.zeros((NT, 64), np.float32)
        for pos, off in ((pr, 0), (pc, 32)):
            ang = pos[:, None] * inv[None, :]
            c, s = np.cos(ang), np.sin(ang)
            cosv[:, off:off + 16] = c
            cosv[:, off + 16:off + 32] = c
            sinv[:, off:off + 16] = -s
            sinv[:, off + 16:off + 32] = s
    else:
        cosv = np.ones((NT, 64), np.float32)
        sinv = np.zeros((NT, 64), np.float32)
    cosv = cosv.reshape(8, 128, 64).transpose(1, 0, 2)
    sinv = sinv.reshape(8, 128, 64).transpose(1, 0, 2)
    out[:, 0] = cosv
    out[:, 1] = sinv
    return out.reshape(128, -1)


def _slots_table(rpb):
    L = rpb.shape[0]
    cq = np.arange(64)
    ck = np.arange(64)
    cstart = np.clip(cq - 8, 0, 48)
    valid = (ck[:, None] >= cstart[None, :]) & (ck[:, None] < cstart[None, :] + 16)
    dc = np.clip(ck[:, None] - cq[None, :] + 15, 0, 30)
    T = rpb[:, :, :, dc]
    T = np.where(valid[None, None, None], T, np.float32(NEG)).astype(np.float32)
    negb = np.full((L, 8, 64, 64), NEG, np.float32)
    out = np.zeros((L, 128, 8, NSLOT, 64), np.float32)
    for s in range(14):
        out[:, 0:64, :, 13 - s, :] = T[:, :, s].transpose(0, 2, 1, 3)
        out[:, 64:128, :, 13 - s, :] = T[:, :, s + 1].transpose(0, 2, 1, 3)
    out[:, 0:64, :, 14, :] = negb.transpose(0, 2, 1, 3)
    out[:, 64:128, :, 14, :] = T[:, :, 3].transpose(0, 2, 1, 3)
    out[:, 0:64, :, 15, :] = T[:, :, 10].transpose(0, 2, 1, 3)
    out[:, 64:128, :, 15, :] = negb.transpose(0, 2, 1, 3)
    out[:, :, :, 16, :] = NEG
    return out.reshape(L, 128, -1)


def _prep(inputs):
    f = lambda a: np.ascontiguousarray(np.asarray(a, dtype=np.float32))
    I = {k: f(v) for k, v in inputs.items()}
    L = DEPTH
    pv = np.zeros((L, 128, NPV), np.float32)
    pv[:, :, 0:48] = I["b_ada"].reshape(L, 48, 128).transpose(0, 2, 1)
    for name, off, n in (("ln1_g", 48, 8), ("ln1_b", 56, 8), ("ln2_g", 64, 8), ("ln2_b", 72, 8),
                         ("conv_b", 80, 4), ("conv_ln_g", 84, 4), ("conv_ln_b", 88, 4)):
        pv[:, :, off:off + n] = I[name].reshape(L, n, 128).transpose(0, 2, 1)
    cd = I["conv_dw"].reshape(L, 31, 4, 128)
    pv[:, :, 92:216] = cd.transpose(0, 3, 2, 1).reshape(L, 128, 124)
    rw = np.zeros((L, 128, 8, 36), np.float32)
    rw[:, :, :, 0:4] = I["rg_w"].reshape(L, 8, 128, 4).transpose(0, 2, 1, 3)
    rw[:, :, :, 4:36] = I["re_w"].reshape(L, 4, 8, 128, 8).transpose(0, 3, 2, 1, 4).reshape(L, 128, 8, 32)
    rb = np.concatenate([I["rg_b"], I["re_b"].reshape(L, 32)], axis=1)
    rb = np.ascontiguousarray(np.broadcast_to(rb[:, None, :], (L, 128, 36)))
    gmw = np.ascontiguousarray(I["gm_ws"].transpose(0, 3, 1, 2)).reshape(L, 128, 512)
    gmbs = np.ascontiguousarray(np.broadcast_to(I["gm_bs"].reshape(L, 1, 512), (L, 128, 512)))
    gmln = np.concatenate([I["gm_ln_g"], I["gm_ln_b"]], axis=1)
    gmln = np.ascontiguousarray(np.broadcast_to(gmln[:, None, :], (L, 128, 1024)))
    slots_s = _slots_table(I["na_rpb"])
    slots_p = np.zeros_like(slots_s)
    ident = np.eye(128, dtype=np.float32)
    sel = np.zeros((128, 32 * 128), np.float32)
    for e in range(32):
        sel[e, e * 128:(e + 1) * 128] = 1.0
    cb = np.concatenate([ident, sel, np.full((128, 128), 1.0 / 1024, np.float32)], axis=1)

    def cf(sample):
        a = np.zeros((128, 387), np.float32)
        a[:, 0:128] = ident
        a[:, 128:256] = 1.0 / 1024
        a[:, 256:384] = 1.0 / 512
        a[:, 384] = 1.0 if sample else 0.0
        a[:, 385] = LN_EPS
        a[:, 386] = 0.0 if sample else NEG
        return a

    def crow(sample):
        a = np.zeros((1, 256), np.float32)
        a[0, 0:128] = 1.0
        a[0, 128:256] = 0.0 if sample else NEG
        return a
    shared = dict(w_ada=I["w_ada"], w_in=I["w_in"], conv_pw=I["conv_pw"], na_out=I["na_out"],
                  gm_out=I["gm_out"], w_o=I["w_o"], moe_w1=I["moe_w1"], moe_w3=I["moe_w3"],
                  moe_w2=I["moe_w2"], pv=pv, rw=rw.reshape(L, 128, 288), rb=rb, gmw=gmw, gmbs=gmbs,
                  gmln=gmln, cb=cb)
    rope_s, rope_p = _rope_tables(True), _rope_tables(False)
    cf_s, cf_p, crow_s, crow_p = cf(True), cf(False), crow(True), crow(False)
    zero_ctx = np.zeros((L, 256, 512), np.float32)
    maps = []
    for core in range(8):
        m = dict(shared)
        if core in (4, 5):
            b = core - 4
            m["x"] = I["x_sample"][b]
            m["cvec"] = np.ascontiguousarray(I["c"][b].reshape(8, 128).T)
            m["ctxk"] = np.ascontiguousarray(I["cache_na_k"][b].reshape(L, 256, 512))
            m["ctxv"] = np.ascontiguousarray(I["cache_na_v"][b].reshape(L, 256, 512))
            m["slots"], m["rope"], m["cf"], m["crow"] = slots_s, rope_s, cf_s, crow_s
        else:
            i = core if core < 4 else core - 6
            m["x"] = np.ascontiguousarray(I["x_prompt"][4 * i:4 * i + 4].reshape(NT, D))
            m["cvec"] = np.ascontiguousarray(I["c_ctx"].reshape(8, 128).T)
            m["ctxk"], m["ctxv"] = zero_ctx, zero_ctx
            m["slots"], m["rope"], m["cf"], m["crow"] = slots_p, rope_p, cf_p, crow_p
        maps.append(m)
    return maps


_NC_CACHE = {}


def kernel(**inputs):
    maps = _prep(inputs)
    if "nc" not in _NC_CACHE:
        _NC_CACHE["nc"] = build_nc()
    nc = _NC_CACHE["nc"]
    res = run_bass_kernel_spmd(nc, maps, core_ids=list(range(8)))
    R = res.results
    y_prompt = np.stack([R[i]["y"] for i in range(4)], 0).reshape(16, 256, D).astype(np.float32)
    y_sample = np.stack([R[4]["y"], R[5]["y"]], 0).astype(np.float32)
    nk = np.zeros((16, DEPTH, 256, 8, 64), np.float32)
    nv = np.zeros((16, DEPTH, 256, 8, 64), np.float32)
    for i in range(4):
        k = R[i]["newk"].reshape(DEPTH, 4, 256, 8, 64).transpose(1, 0, 2, 3, 4)
        v = R[i]["newv"].reshape(DEPTH, 4, 256, 8, 64).transpose(1, 0, 2, 3, 4)
        nk[4 * i:4 * i + 4] = k
        nv[4 * i:4 * i + 4] = v
    return (y_prompt, y_sample, nk, nv)
```
